# Optimizing a Trainium2 kernel written in Bass

```python
import math
import jax, jax.numpy as jnp
from jax import lax
import numpy as np

D_MODEL = 1024
BATCH = 4
SEQ = 4096
DEPTH = 1

N_MEM = 256
SB_HEADS = 8
SB_HEAD_DIM = 64
SB_WIDTH = SB_HEADS * SB_HEAD_DIM
ML_HEADS = 4
ML_HEAD_DIM = 128
ML_WIDTH = ML_HEADS * ML_HEAD_DIM
ML_CONV = 4
ML_CHUNK = 128
XA_HEADS = 4
XA_HEAD_DIM = 128
XA_WIDTH = XA_HEADS * XA_HEAD_DIM
N_BRANCH = 3
D_FF = 2816
FF_CONV = 3
Q_BLOCK = 128
EPS = 1e-6

IN_SIZES = (SB_WIDTH, SB_WIDTH, SB_WIDTH, ML_WIDTH, ML_WIDTH, ML_WIDTH, ML_WIDTH,
            ML_HEADS, ML_HEADS, XA_WIDTH, N_BRANCH * D_MODEL)
IN_TOTAL = sum(IN_SIZES)

kernel_name = 'stickbreak_mlstm_memxattn_convffn_hybrid'


def _rmsnorm(x, g):
    xf = x.astype(jnp.float32)
    y = xf * lax.rsqrt(jnp.mean(xf * xf, axis=-1, keepdims=True) + EPS)
    return (y * g.astype(jnp.float32)).astype(x.dtype)


def _causal_dwconv(x, w, b):
    k_width, chans = w.shape
    y = lax.conv_general_dilated(
        x, w[:, None, :].astype(x.dtype), window_strides=(1,),
        padding=[(k_width - 1, 0)], dimension_numbers=('NWC', 'WIO', 'NWC'),
        feature_group_count=chans)
    return y + b.astype(x.dtype)


def _split_heads(t, n_heads):
    b, s, _ = t.shape
    return t.reshape(b, s, n_heads, -1).transpose(0, 2, 1, 3)


def _merge_heads(t):
    b, h, s, d = t.shape
    return t.transpose(0, 2, 1, 3).reshape(b, s, h * d)


def _stick_breaking_attention(q, k, v):
    seq, dh = q.shape[2], q.shape[3]
    scale = 1.0 / math.sqrt(dh)
    q_pos = jnp.arange(Q_BLOCK)
    outs = []
    for blk in range(seq // Q_BLOCK):
        t0 = blk * Q_BLOCK
        t1 = t0 + Q_BLOCK
        z = jnp.einsum('bhtd,bhsd->bhts', q[:, :, t0:t1], k[:, :, :t1]).astype(jnp.float32) * scale
        strict = jnp.arange(t1)[None, :] < (t0 + q_pos)[:, None]
        log_fail = jnp.where(strict, jax.nn.log_sigmoid(-z), 0.0)
        log_after = lax.cumsum(log_fail, axis=3, reverse=True) - log_fail
        w = jnp.where(strict, jnp.exp(jax.nn.log_sigmoid(z) + log_after), 0.0)
        outs.append(jnp.einsum('bhts,bhsd->bhtd', w.astype(v.dtype), v[:, :, :t1]))
    return jnp.concatenate(outs, axis=2)


def _mlstm_chunkwise(q, k, v, i_pre, f_pre):
    bsz, nh, seq, dh = q.shape
    n_chunks = seq // ML_CHUNK
    k = k * (1.0 / math.sqrt(dh))

    def chunked(t):
        t = t.reshape(bsz, nh, n_chunks, ML_CHUNK, *t.shape[3:])
        return jnp.moveaxis(t, 2, 0)

    log_f = jax.nn.log_sigmoid(f_pre)
    causal = jnp.tril(jnp.ones((ML_CHUNK, ML_CHUNK), dtype=bool))

    def step(carry, inp):
        c_st, n_st, m_st = carry
        qc, kc, vc, ic, lfc = inp
        b = jnp.cumsum(lfc, axis=-1)
        log_d = jnp.where(causal, b[..., :, None] - b[..., None, :] + ic[..., None, :], -jnp.inf)
        m_inter = b + m_st[..., None]
        m_t = jnp.maximum(m_inter, jnp.max(log_d, axis=-1))
        s = jnp.einsum('bhtd,bhsd->bhts', qc, kc) * jnp.exp(log_d - m_t[..., None])
        w_inter = jnp.exp(m_inter - m_t)
        num = (jnp.einsum('bhts,bhsd->bhtd', s, vc)
               + w_inter[..., None] * jnp.einsum('bhvk,bhtk->bhtv', c_st, qc))
        den = jnp.sum(s, axis=-1) + w_inter * jnp.einsum('bhk,bhtk->bht', n_st, qc)
        h = num / jnp.maximum(jnp.abs(den), jnp.exp(-m_t))[..., None]
        b_last = b[..., -1]
        log_g = b_last[..., None] - b + ic
        m_new = jnp.maximum(b_last + m_st, jnp.max(log_g, axis=-1))
        decay = jnp.exp(b_last + m_st - m_new)
        wk = jnp.exp(log_g - m_new[..., None])
        c_new = decay[..., None, None] * c_st + jnp.einsum('bhs,bhsv,bhsk->bhvk', wk, vc, kc)
        n_new = decay[..., None] * n_st + jnp.einsum('bhs,bhsk->bhk', wk, kc)
        return (c_new, n_new, m_new), h

    init = (jnp.zeros((bsz, nh, dh, dh), jnp.float32),
            jnp.zeros((bsz, nh, dh), jnp.float32),
            jnp.zeros((bsz, nh), jnp.float32))
    _, h = lax.scan(step, init, (chunked(q), chunked(k), chunked(v), chunked(i_pre), chunked(log_f)))
    return jnp.moveaxis(h, 0, 2).reshape(bsz, nh, seq, dh)


def _memory_cross_attention(q, mem_n, w_kv, g_q, g_k):
    k, v = jnp.split(mem_n @ w_kv, 2, axis=-1)
    qh = _rmsnorm(_split_heads(q, XA_HEADS), g_q)
    kh = _rmsnorm(_split_heads(k, XA_HEADS), g_k)
    vh = _split_heads(v, XA_HEADS)
    scores = jnp.einsum('bhsd,bhmd->bhsm', qh, kh).astype(jnp.float32) * (1.0 / math.sqrt(XA_HEAD_DIM))
    p = jax.nn.softmax(scores, axis=-1)
    return _merge_heads(jnp.einsum('bhsm,bhmd->bhsd', p.astype(vh.dtype), vh))


def setup_inputs(seed: int = 0) -> dict:
    key = jax.random.key(seed)
    ks = jax.random.split(key, 24)

    def nrm(k, shape, scale):
        return jax.random.normal(k, shape, jnp.float32) * scale

    def gain(k, shape):
        return 1.0 + 0.02 * jax.random.normal(k, shape, jnp.float32)

    return {
        'x': nrm(ks[0], (BATCH, SEQ, D_MODEL), 1.0),
        'mem': nrm(ks[1], (BATCH, N_MEM, D_MODEL), 1.0),
        'g_mix': gain(ks[2], (DEPTH, D_MODEL)),
        'w_in': nrm(ks[3], (DEPTH, D_MODEL, IN_TOTAL), D_MODEL ** -0.5),
        'ml_conv_w': nrm(ks[4], (DEPTH, ML_CONV, 2 * ML_WIDTH), ML_CONV ** -0.5),
        'ml_conv_b': nrm(ks[5], (DEPTH, 2 * ML_WIDTH), 0.01),
        'ml_b_i': nrm(ks[6], (DEPTH, ML_HEADS), 0.1),
        'ml_b_f': jnp.linspace(3.0, 6.0, ML_HEADS, dtype=jnp.float32)[None, :] + nrm(ks[7], (DEPTH, ML_HEADS), 0.1),
        'ml_g_out': gain(ks[8], (DEPTH, ML_HEADS, ML_HEAD_DIM)),
        'g_mem': gain(ks[9], (DEPTH, D_MODEL)),
        'w_mem_kv': nrm(ks[10], (DEPTH, D_MODEL, 2 * XA_WIDTH), D_MODEL ** -0.5),
        'xa_g_q': gain(ks[11], (DEPTH, XA_HEAD_DIM)),
        'xa_g_k': gain(ks[12], (DEPTH, XA_HEAD_DIM)),
        'w_sb_out': nrm(ks[13], (DEPTH, SB_WIDTH, D_MODEL), SB_WIDTH ** -0.5),
        'w_ml_out': nrm(ks[14], (DEPTH, ML_WIDTH, D_MODEL), ML_WIDTH ** -0.5),
        'w_xa_out': nrm(ks[15], (DEPTH, XA_WIDTH, D_MODEL), XA_WIDTH ** -0.5),
        'w_o': nrm(ks[16], (DEPTH, D_MODEL, D_MODEL), D_MODEL ** -0.5),
        'g_ffn': gain(ks[17], (DEPTH, D_MODEL)),
        'w_up': nrm(ks[18], (DEPTH, D_MODEL, 2 * D_FF), D_MODEL ** -0.5),
        'ff_conv_w': nrm(ks[19], (DEPTH, FF_CONV, 2 * D_FF), FF_CONV ** -0.5),
        'ff_conv_b': nrm(ks[20], (DEPTH, 2 * D_FF), 0.01),
        'w_down': nrm(ks[21], (DEPTH, D_FF, D_MODEL), D_FF ** -0.5),
    }


def reference(x, mem, g_mix, w_in, ml_conv_w, ml_conv_b, ml_b_i, ml_b_f, ml_g_out, g_mem,
              w_mem_kv, xa_g_q, xa_g_k, w_sb_out, w_ml_out, w_xa_out, w_o, g_ffn, w_up,
              ff_conv_w, ff_conv_b, w_down):
    bsz, seq, _ = x.shape
    offsets = np.cumsum(IN_SIZES)[:-1].tolist()
    for l in range(DEPTH):
        h = _rmsnorm(x, g_mix[l])
        (sb_q, sb_k, sb_v, ml_q, ml_k, ml_v, ml_o, ml_i, ml_f, xa_q, gate_pre) = jnp.split(
            h @ w_in[l], offsets, axis=-1)

        y_sb = _merge_heads(_stick_breaking_attention(
            _split_heads(sb_q, SB_HEADS), _split_heads(sb_k, SB_HEADS), _split_heads(sb_v, SB_HEADS)))

        ml_qk = jax.nn.silu(_causal_dwconv(jnp.concatenate([ml_q, ml_k], axis=-1), ml_conv_w[l], ml_conv_b[l]))
        ml_q, ml_k = jnp.split(ml_qk, 2, axis=-1)
        i_pre = (ml_i + ml_b_i[l]).astype(jnp.float32).transpose(0, 2, 1)
        f_pre = (ml_f + ml_b_f[l]).astype(jnp.float32).transpose(0, 2, 1)
        h_ml = _mlstm_chunkwise(
            _split_heads(ml_q, ML_HEADS).astype(jnp.float32),
            _split_heads(ml_k, ML_HEADS).astype(jnp.float32),
            _split_heads(ml_v, ML_HEADS).astype(jnp.float32), i_pre, f_pre)
        h_ml = _merge_heads(h_ml).astype(x.dtype).reshape(bsz, seq, ML_HEADS, ML_HEAD_DIM)
        h_ml = _rmsnorm(h_ml, ml_g_out[l]).reshape(bsz, seq, ML_WIDTH)
        y_ml = jax.nn.sigmoid(ml_o) * h_ml

        y_xa = _memory_cross_attention(xa_q, _rmsnorm(mem, g_mem[l]), w_mem_kv[l], xa_g_q[l], xa_g_k[l])

        gates = jax.nn.sigmoid(gate_pre).reshape(bsz, seq, N_BRANCH, D_MODEL)
        merged = (gates[:, :, 0] * (y_sb @ w_sb_out[l])
                  + gates[:, :, 1] * (y_ml @ w_ml_out[l])
                  + gates[:, :, 2] * (y_xa @ w_xa_out[l]))
        x = x + merged @ w_o[l]

        h = _rmsnorm(x, g_ffn[l])
        u = _causal_dwconv(h @ w_up[l], ff_conv_w[l], ff_conv_b[l])
        u_val, u_gate = jnp.split(u, 2, axis=-1)
        x = x + (jax.nn.silu(u_gate) * u_val) @ w_down[l]
    return x
```

```python
import numpy as np
import concourse.bass as bass
import concourse.mybir as mybir
from concourse.bass_utils import run_bass_kernel_spmd

F32 = mybir.dt.float32
BF16 = mybir.dt.bfloat16
AF = mybir.ActivationFunctionType
ALU = mybir.AluOpType
AX = mybir.AxisListType

D = 1024
SEQ = 4096
NT = 32
NP = 15
NQ = 17
TOK = NT * 128
OWN0 = NP * 128
NOWN = NQ * 128
DFF = 2816
NMEM = 256
EPS = 1e-6
NEG = -30000.0

C_SBQ, C_SBK, C_SBV = 0, 512, 1024
C_MLQ, C_MLK, C_MLV, C_MLO = 1536, 2048, 2560, 3072
C_MLI, C_MLF = 3584, 3588
C_XAQ = 3592
C_GATE = 4104

SAME_ENGINE_RAW = True
STRICT_SYNC = True


class Buf:
    __slots__ = ("name", "w", "r")

    def __init__(self, name=""):
        self.name = name
        self.w = None
        self.r = {}


class Tracker:
    ND = 6

    def __init__(self, nc):
        self.nc = nc
        self.E = dict(pe=nc.tensor, act=nc.scalar, dve=nc.vector, pool=nc.gpsimd, sp=nc.sync)
        self.sem = {k: nc.alloc_semaphore("s_" + k) for k in ("pe", "act", "dve", "pool")}
        self.cnt = {k: 0 for k in self.sem}
        self.seen = {k: {} for k in self.E}
        self.dsem = {}
        self.dcnt = {}
        self.dnext = {}
        for q in ("sp", "pool", "act"):
            self.dsem[q] = [nc.alloc_semaphore(f"d_{q}{i}") for i in range(self.ND)]
            self.dcnt[q] = [0] * self.ND
            self.dnext[q] = 0
        self.bufs = {}
        self.nwait = 0

    def buf(self, key):
        b = self.bufs.get(key)
        if b is None:
            b = self.bufs[key] = Buf(str(key))
        return b

    def _wait(self, eng, tok):
        sem, val, key = tok
        if self.seen[eng].get(key, 0) >= val:
            return
        self.E[eng].wait_ge(sem, val)
        self.seen[eng][key] = val
        self.nwait += 1

    def _deps(self, eng, reads, writes):
        for b in reads:
            if b.w is not None:
                if b.w[2] == eng and (eng == "pe" or not SAME_ENGINE_RAW):
                    continue
                self._wait(eng, b.w)
        for b in writes:
            if b.w is not None and (b.w[2] != eng or (STRICT_SYNC and eng != "pe")):
                self._wait(eng, b.w)
            for k, tok in b.r.items():
                if k != eng or (STRICT_SYNC and eng != "pe"):
                    self._wait(eng, tok)

    def _mark(self, tok, reads, writes):
        for b in reads:
            b.r[tok[2]] = tok
        for b in writes:
            b.w = tok
            b.r = {}

    def op(self, eng, fn, reads=(), writes=()):
        reads = [self.buf(b) if not isinstance(b, Buf) else b for b in reads]
        writes = [self.buf(b) if not isinstance(b, Buf) else b for b in writes]
        self._deps(eng, reads, writes)
        ins = fn(self.E[eng])
        self.cnt[eng] += 1
        ins.then_inc(self.sem[eng], 1)
        tok = (self.sem[eng], self.cnt[eng], eng)
        self._mark(tok, reads, writes)
        return tok

    def dma(self, out, in_, reads=(), writes=(), q="sp"):
        reads = [self.buf(b) if not isinstance(b, Buf) else b for b in reads]
        writes = [self.buf(b) if not isinstance(b, Buf) else b for b in writes]
        i = self.dnext[q]
        self.dnext[q] = (i + 1) % self.ND
        key = f"d_{q}{i}"
        if self.dcnt[q][i] > 0:
            self._wait(q, (self.dsem[q][i], 16 * self.dcnt[q][i], key))
        self._deps(q, reads, writes)
        ins = self.E[q].dma_start(out=out, in_=in_)
        self.dcnt[q][i] += 1
        ins.then_inc(self.dsem[q][i], 16)
        tok = (self.dsem[q][i], 16 * self.dcnt[q][i], key)
        self._mark(tok, reads, writes)
        return tok

    def barrier(self):
        toks = [(self.sem[k], self.cnt[k], k) for k in self.sem if self.cnt[k] > 0]
        for q in self.dsem:
            for i in range(self.ND):
                if self.dcnt[q][i] > 0:
                    toks.append((self.dsem[q][i], 16 * self.dcnt[q][i], f"d_{q}{i}"))
        for eng in self.E:
            for tok in toks:
                if tok[2] == eng:
                    continue
                self._wait(eng, tok)

    def finish(self):
        toks = []
        for q in self.dsem:
            for i in range(self.ND):
                if self.dcnt[q][i] > 0:
                    toks.append((self.dsem[q][i], 16 * self.dcnt[q][i], f"d_{q}{i}"))
        for tok in toks:
            self._wait("sp", tok)
        self.E["sp"].nop()


class Arena:
    def __init__(self, nc, lo=16512, hi=229344):
        self.nc = nc
        self.lo = lo
        self.hi = hi
        self.hi0 = hi
        self.top = lo
        self.n = 0

    def alloc(self, name, shape, dtype):
        per = 1
        for s in shape[1:]:
            per *= s
        nbytes = per * (4 if dtype == F32 else 2)
        off = (self.top + 63) // 64 * 64
        assert off + nbytes <= self.hi, f"SBUF overflow allocating {name}: {off}+{nbytes} > {self.hi}"
        self.top = off + nbytes
        self.n += 1
        return self.nc.alloc_sbuf_tensor_at(f"{name}_{self.n}", list(shape), dtype, offset=off)

    def alloc_top(self, name, shape, dtype):
        per = 1
        for s in shape[1:]:
            per *= s
        nbytes = per * (4 if dtype == F32 else 2)
        off = (self.hi - nbytes) // 64 * 64
        assert off >= self.top, f"SBUF overflow (top) allocating {name}"
        self.hi = off
        self.n += 1
        return self.nc.alloc_sbuf_tensor_at(f"{name}_{self.n}", list(shape), dtype, offset=off)

    def mark(self):
        return self.top

    def release(self, m):
        self.top = m


def build_program(debug=None, upto="all"):
    debug = debug or []
    nc = bass.Bass("TRN2", target_bir_lowering=False)
    T = Tracker(nc)
    A = Arena(nc)
    op, dma = T.op, T.dma
    PH = ["A", "B", "C", "D", "E", "F", "all"]
    upi = PH.index(upto)

    def din(name, shape, dt=F32):
        return nc.dram_tensor(name, list(shape), dt, kind="ExternalInput").ap()

    xs = din("xs", [TOK, D])
    mem = din("mem", [NMEM, D])
    w_in = din("w_in", [D, 7176])
    g_mix_bc = din("g_mix_bc", [128, D])
    kbias_d = din("kbias", [128, NT])
    ident_d = din("ident", [128, 128])
    cmask_d = din("cmasks", [128, 4 * 128])
    rowmask_d = din("rowmask", [128, 2])
    g_mem_bc = din("g_mem_bc", [128, D])
    xa_g_d = din("xa_g", [128, 2])
    w_kv = din("w_mem_kv", [D, 1024])
    wg_d = w_in[:, C_MLI:C_MLI + 8]
    gb_d = din("gb", [4, 2])
    ifb_d = din("ifbias", [4, 2 * TOK])
    cw_d = din("cw", [128, 8 * 5])
    gout_d = din("gout_bc", [128, 512])
    wbr_d = [din("w_sb_out", [512, D]), din("w_ml_out", [512, D]), din("w_xa_out", [512, D])]
    w_o_d = din("w_o", [D, D])
    g_ffn_bc = din("g_ffn_bc", [128, D])
    valid_d = din("valid", [128, NQ])
    w_up = din("w_up", [D, 2 * DFF])
    cf_d = din("cf", [128, 44 * 4])
    w_down = din("w_down", [DFF, D])
    x1s = nc.dram_tensor("x1s", [NOWN, D], F32).ap()
    out_d = nc.dram_tensor("out", [16 * 128, D], F32, kind="ExternalOutput").ap()
    dbg = {}

    def dbg_out(name, shape, dt=F32):
        dbg[name] = nc.dram_tensor("dbg_" + name, list(shape), dt, kind="ExternalOutput").ap()
        return dbg[name]

    ps = [nc.alloc_psum_tensor(f"ps{i}", [128, 512], F32) for i in range(6)]
    pb = [nc.alloc_psum_tensor(f"pb{i}", [128, 8, 128], BF16) for i in range(2)]
    PS = lambda i: ("ps", i)
    PB = lambda i: ("pb", i)

    ident_f = A.alloc("ident_f", [128, 128], F32)
    ident_b = A.alloc("ident_b", [128, 128], BF16)
    cm_f = A.alloc("cm_f", [128, 512], F32)
    cm_b = A.alloc("cm_b", [128, 512], BF16)
    mask4 = A.alloc("mask4", [128, 512], F32)
    ones_b = A.alloc("ones_b", [128, 128], BF16)
    eps_t = A.alloc("eps", [128, 1], F32)
    one_t = A.alloc("one", [128, 1], F32)
    kbias = A.alloc("kbias", [128, NT], F32)
    dma(ident_f[:], ident_d, writes=["ident_f"])
    dma(cm_f[:], cmask_d, writes=["cm_f"])
    dma(kbias[:], kbias_d, writes=["kbias"])
    rowmask = A.alloc("rowmask", [128, 2], F32)
    dma(rowmask[:], rowmask_d, writes=["rowmask"])
    op("pool", lambda e: e.memset(eps_t[:], EPS), writes=["eps"])
    op("pool", lambda e: e.memset(one_t[:], 1.0), writes=["one"])
    op("pool", lambda e: e.memset(ones_b[:], 1.0), writes=["ones_b"])
    op("dve", lambda e: e.tensor_copy(out=ident_b[:], in_=ident_f[:]), reads=["ident_f"], writes=["ident_b"])
    op("dve", lambda e: e.tensor_copy(out=cm_b[:], in_=cm_f[:]), reads=["cm_f"], writes=["cm_b"])
    for r in range(4):
        op("dve", lambda e: e.tensor_copy(out=mask4[:, r * 128:(r + 1) * 128], in_=cm_f[:, 0:128]),
           reads=["cm_f"], writes=["mask4"])
    m_strict = cm_f[:, 0:128]
    m_incl = cm_f[:, 128:256]
    Tm = cm_b[:, 256:384]
    Um = cm_b[:, 384:512]

    mConst = A.mark()
    class WStage:
        def __init__(self, name, nkc, ncols, nbuf=2):
            self.name, self.nkc, self.ncols, self.nbuf = name, nkc, ncols, nbuf
            self.b = [A.alloc(f"{name}_b{i}", [128, nkc, ncols], BF16) for i in range(nbuf)]
            self.i = 0

        def load(self, src_ap, ncols=None, nkc=None):
            ncols = ncols or self.ncols
            nkc = nkc or self.nkc
            i = self.i
            self.i = (i + 1) % self.nbuf
            bk = (self.name, "b", i)
            dma(self.b[i][:, :nkc, :ncols], src_ap.rearrange("(kc p) n -> p kc n", p=128), writes=[bk], q="pool")
            return self.b[i], bk

    evac_rr = [0]

    def evac_copy(out_ap, in_ap, reads, writes):
        evac_rr[0] ^= 1
        if evac_rr[0]:
            op("act", lambda e: e.copy(out=out_ap, in_=in_ap), reads=reads, writes=writes)
        else:
            op("dve", lambda e: e.tensor_copy(out=out_ap, in_=in_ap), reads=reads, writes=writes)

    ps_rr = [0]

    def next_ps(n=6):
        ps_rr[0] = (ps_rr[0] + 1) % n
        return ps_rr[0]

    def tok_tiles(n0, n):
        o = 0
        while o < n:
            w = min(512, n - o)
            yield n0 + o, o, w
            o += w

    def proj_fm(wb, wkey, nkc, col0, inT, in_keys, tok0, ntok, evac):
        for (ta, o, w) in tok_tiles(tok0, ntok):
            pi = next_ps()
            for kc in range(nkc):
                op("pe", lambda e: e.matmul(ps[pi][:, :w], lhsT=wb[:, kc, col0:col0 + 128], rhs=inT[:, kc, ta:ta + w],
                                            start=(kc == 0), stop=(kc == nkc - 1)),
                   reads=[wkey] + in_keys(ta, w), writes=[PS(pi)])
            evac(o, w, ps[pi][:, :w], PS(pi))

    def hT_keys(ta, w):
        return [("hT", t) for t in range(ta // 128, (ta + w - 1) // 128 + 1)]

    hT = A.alloc("hT", [128, 8, TOK], BF16)
    mA = A.mark()
    gmix = A.alloc("gmix", [128, D], F32)
    dma(gmix[:], g_mix_bc, writes=["gmix"])
    xin = [A.alloc(f"xin{i}", [128, D], F32) for i in range(4)]
    hb = [A.alloc(f"hb{i}", [128, D], BF16) for i in range(4)]
    junk = A.alloc("junk", [128, D], BF16)
    ss = A.alloc("ss", [128, NT], F32)
    rs = A.alloc("rs", [128, NT], F32)
    rstd = A.alloc("rstd", [128, NT], F32)
    def a_front(t):
        i = t % 4
        X, H, P = xin[i], hb[i], pb[t % 2]
        dma(X[:], xs[t * 128:(t + 1) * 128, :], writes=[f"xin{i}"], q=("sp" if t % 2 == 0 else "pool"))
        op("act", lambda e: e.activation(out=junk[:], in_=X[:], func=AF.Square, accum_out=ss[:, t:t + 1]),
           reads=[f"xin{i}"], writes=[("ss", t)])
        op("act", lambda e: e.activation(out=rs[:, t:t + 1], in_=ss[:, t:t + 1], func=AF.Sqrt,
                                         scale=1.0 / D, bias=eps_t[:]),
           reads=[("ss", t), "eps"], writes=[("rs", t)])
        op("dve", lambda e: e.reciprocal(out=rstd[:, t:t + 1], in_=rs[:, t:t + 1]),
           reads=[("rs", t)], writes=[("rstd", t)])
        op("dve", lambda e: e.scalar_tensor_tensor(out=H[:], in0=X[:], scalar=rstd[:, t:t + 1], in1=gmix[:],
                                                   op0=ALU.mult, op1=ALU.mult),
           reads=[f"xin{i}", ("rstd", t), "gmix"], writes=[f"hb{i}"])
        for kc in range(8):
            op("pe", lambda e: e.transpose(out=P[:, kc, :], in_=H[:, kc * 128:(kc + 1) * 128], identity=ident_b[:]),
               reads=[f"hb{i}", "ident_b"], writes=[PB(t % 2)])

    def a_back(t):
        evac_copy(hT[:, :, t * 128:(t + 1) * 128], pb[t % 2][:], [PB(t % 2)], [("hT", t)])

    for t in range(NT + 1):
        if t < NT:
            a_front(t)
        if t >= 1:
            a_back(t - 1)
    T.barrier()
    A.release(mA)

    if "hT" in debug:
        d = dbg_out("hT", [128, 8 * TOK], BF16)
        dma(d, hT[:].rearrange("p a b -> p (a b)"), reads=[("hT", t) for t in range(NT)])

    ysbT = A.alloc("ysbT", [128, 4, NOWN], BF16)

    if upi >= 1:
        mB = A.mark()
        ws = WStage("wsB", 8, 256, nbuf=2)
        kT = A.alloc("kT", [128, 2, TOK], BF16)
        qT = A.alloc("qT", [128, 4, NOWN], BF16)
        vS = A.alloc("vS", [128, NT, 256], BF16)
        Et = [[A.alloc(f"E{s}{i}", [128, 512], F32) for i in range(2)] for s in range(2)]
        Lt = [[A.alloc(f"L{s}{i}", [128, 512], BF16) for i in range(2)] for s in range(2)]
        Xt = [[A.alloc(f"X{s}{i}", [128, 512], F32) for i in range(2)] for s in range(2)]
        Wt = [[A.alloc(f"W{s}{i}", [128, 512], BF16) for i in range(2)] for s in range(2)]
        for hg in range(2):
            wb, wk = ws.load(w_in[:, C_SBK + hg * 256: C_SBK + (hg + 1) * 256])
            for pr in range(2):
                proj_fm(wb, wk, 8, pr * 128, hT, hT_keys, 0, TOK,
                        lambda o, w, p, pk, pr=pr: evac_copy(kT[:, pr, o:o + w], p, [pk],
                                                             [("kT", pr, t) for t in range(o // 128, (o + w) // 128)]))
            wb, wk = ws.load(w_in[:, C_SBQ + hg * 256: C_SBQ + (hg + 1) * 256])
            for pr in range(2):
                def q_evac(o, w, p, pk, pr=pr):
                    wr = [("qT", pr, t) for t in range(o // 128, (o + w) // 128)]
                    op("act", lambda e: e.activation(out=qT[:, pr, o:o + w], in_=p, func=AF.Copy, scale=rowmask[:, 0:1]),
                       reads=[pk, "rowmask"], writes=wr)
                    op("dve", lambda e: e.tensor_scalar(out=qT[:, 2 + pr, o:o + w], in0=p, scalar1=rowmask[:, 1:2], scalar2=None,
                                                        op0=ALU.mult),
                       reads=[pk, "rowmask"], writes=wr)
                proj_fm(wb, wk, 8, pr * 128, hT, hT_keys, OWN0, NOWN, q_evac)
            wb, wk = ws.load(w_in[:, C_SBV + hg * 256: C_SBV + (hg + 1) * 256])
            for t in range(NT):
                pi = next_ps()
                for kc in range(8):
                    op("pe", lambda e: e.matmul(ps[pi][:, :256], lhsT=hT[:, kc, t * 128:(t + 1) * 128], rhs=wb[:, kc, :256],
                                                start=(kc == 0), stop=(kc == 7)),
                       reads=[wk, ("hT", t)], writes=[PS(pi)])
                evac_copy(vS[:, t, :], ps[pi][:, :256], [PS(pi)], [("vS", t)])
            NCH = 2
            order = list(range(NQ - 1, -1, -1))

            def emit_z(s, qb, kb):
                for h4 in range(4):
                    pr, half = h4 // 2, h4 % 2
                    op("pe", lambda e: e.matmul(ps[s][:, h4 * 128:(h4 + 1) * 128],
                                                lhsT=kT[:, pr, kb * 128:(kb + 1) * 128],
                                                rhs=qT[:, half * 2 + pr, qb * 128:(qb + 1) * 128],
                                                start=(h4 == 0), stop=True, skip_group_check=True),
                       reads=[("kT", pr, kb), ("qT", pr, qb)], writes=[PS(s)])

            cur = [None] * NCH
            for s in range(NCH):
                if order:
                    qb = order.pop(0)
                    cur[s] = (qb, NP + qb)
                    emit_z(s, *cur[s])
            step = 0
            while any(c is not None for c in cur):
                act_ = [s for s in range(NCH) if cur[s] is not None]
                bi = step % 2
                step += 1
                info = {}
                nxt = [None] * NCH
                for s in act_:
                    qb, kb = cur[s]
                    gq = NP + qb
                    info[s] = (qb, kb, gq, kb == gq, kb == 0)
                    if kb > 0:
                        nxt[s] = (qb, kb - 1)
                    elif order:
                        q2 = order.pop(0)
                        nxt[s] = (q2, NP + q2)
                for s in act_:
                    qb, kb, gq, first, last = info[s]
                    E_ = Et[s][bi]
                    op("act", lambda e: e.activation(out=E_[:], in_=ps[s][:], func=AF.Exp, scale=0.125,
                                                     bias=kbias[:, kb:kb + 1]),
                       reads=[PS(s), "kbias"], writes=[("E", s, bi)])
                for s in act_:
                    qb, kb, gq, first, last = info[s]
                    E_ = Et[s][bi]
                    if first:
                        op("dve", lambda e: e.tensor_tensor(out=E_[:], in0=E_[:], in1=mask4[:], op=ALU.mult),
                           reads=[("E", s, bi), "mask4"], writes=[("E", s, bi)])
                for s in act_:
                    E_, L_ = Et[s][bi], Lt[s][bi]
                    op("act", lambda e: e.activation(out=L_[:], in_=E_[:], func=AF.Ln, scale=1.0, bias=one_t[:]),
                       reads=[("E", s, bi), "one"], writes=[("L", s, bi)])
                for s in act_:
                    qb, kb, gq, first, last = info[s]
                    L_ = Lt[s][bi]
                    ci = 2 + s
                    op("pe", lambda e: e.matmul(ps[ci][:], lhsT=Tm, rhs=L_[:], start=first, stop=True,
                                                skip_group_check=True),
                       reads=[("L", s, bi), "cm_b"], writes=[PS(ci)])
                for s in range(NCH):
                    if nxt[s] is not None:
                        emit_z(s, *nxt[s])
                for s in act_:
                    X_ = Xt[s][bi]
                    ci = 2 + s
                    op("act", lambda e: e.activation(out=X_[:], in_=ps[ci][:], func=AF.Exp, scale=-1.0),
                       reads=[PS(ci)], writes=[("X", s, bi)])
                for s in act_:
                    qb, kb, gq, first, last = info[s]
                    L_ = Lt[s][bi]
                    ci = 2 + s
                    if not last:
                        op("pe", lambda e: e.matmul(ps[ci][:], lhsT=Um, rhs=L_[:], start=False, stop=True,
                                                    skip_group_check=True),
                           reads=[("L", s, bi), "cm_b"], writes=[PS(ci)])
                for s in act_:
                    E_, X_, W_ = Et[s][bi], Xt[s][bi], Wt[s][bi]
                    op("dve", lambda e: e.tensor_tensor(out=W_[:], in0=E_[:], in1=X_[:], op=ALU.mult),
                       reads=[("E", s, bi), ("X", s, bi)], writes=[("W", s, bi)])
                for s in act_:
                    qb, kb, gq, first, last = info[s]
                    W_ = Wt[s][bi]
                    yi = 4 + s
                    for h4 in range(4):
                        pr = h4 // 2
                        op("pe", lambda e: e.matmul(ps[yi][:, h4 * 128:(h4 + 1) * 128],
                                                    lhsT=vS[:, kb, pr * 128:(pr + 1) * 128],
                                                    rhs=W_[:, h4 * 128:(h4 + 1) * 128],
                                                    start=(first and h4 == 0), stop=True, skip_group_check=True),
                           reads=[("vS", kb), ("W", s, bi)], writes=[PS(yi)])
                for s in act_:
                    qb, kb, gq, first, last = info[s]
                    yi = 4 + s
                    if last:
                        for h4 in range(4):
                            pr, half = h4 // 2, h4 % 2
                            rows = slice(half * 64, half * 64 + 64)
                            op("dve", lambda e: e.tensor_copy(out=ysbT[rows, hg * 2 + pr, qb * 128:(qb + 1) * 128],
                                                              in_=ps[yi][rows, h4 * 128:(h4 + 1) * 128]),
                               reads=[PS(yi)], writes=[("ysbT", hg * 2 + pr, qb, half)])
                cur = nxt
            T.barrier()
        A.release(mB)
        if "ysbT" in debug:
            d = dbg_out("ysbT", [128, 4 * NOWN], BF16)
            dma(d, ysbT[:].rearrange("p a b -> p (a b)"), reads=[])

    def norm_block(X, xkey, H, hkey, t, gbc, gkey, ssT, rsT, rstdT, tag, valid_ap=None):
        op("act", lambda e: e.activation(out=junk2[:], in_=X[:], func=AF.Square, accum_out=ssT[:, t:t + 1]),
           reads=[xkey], writes=[(tag, "ss", t)])
        op("act", lambda e: e.activation(out=rsT[:, t:t + 1], in_=ssT[:, t:t + 1], func=AF.Sqrt,
                                         scale=1.0 / D, bias=eps_t[:]),
           reads=[(tag, "ss", t), "eps"], writes=[(tag, "rs", t)])
        op("dve", lambda e: e.reciprocal(out=rstdT[:, t:t + 1], in_=rsT[:, t:t + 1]),
           reads=[(tag, "rs", t)], writes=[(tag, "rstd", t)])
        if valid_ap is not None:
            op("dve", lambda e: e.tensor_tensor(out=rstdT[:, t:t + 1], in0=rstdT[:, t:t + 1], in1=valid_ap, op=ALU.mult),
               reads=[(tag, "rstd", t), "valid"], writes=[(tag, "rstd", t)])
        op("dve", lambda e: e.scalar_tensor_tensor(out=H[:], in0=X[:], scalar=rstdT[:, t:t + 1], in1=gbc[:],
                                                   op0=ALU.mult, op1=ALU.mult),
           reads=[xkey, (tag, "rstd", t), gkey], writes=[hkey])

    if upi >= 2:
        ymlT = A.alloc("ymlT", [128, 4, NOWN], BF16)
        mC0 = A.mark()
        gT = A.alloc("gT", [128, NT, 16], F32)
        cw = A.alloc("cw", [128, 8, 5], F32)
        gout = A.alloc("gout", [128, 512], F32)
        dma(cw[:].rearrange("p a b -> p (a b)"), cw_d, writes=["cw"])
        dma(gout[:], gout_d, writes=["gout"])
        mC = A.mark()
        LNS = float(np.log(1.0 / np.sqrt(128.0)))
        G = [A.alloc(f"G{i}", [4, TOK], F32) for i in range(5)]
        wgs = WStage("wsg", 8, 8, nbuf=1)
        gb = A.alloc("gb", [4, 2], F32)
        ngb = A.alloc("ngb", [4, 2], F32)
        lns_t = A.alloc("lns", [4, 1], F32)
        ones4 = A.alloc("ones4", [4, 128], F32)
        amax = A.alloc("amax", [4, NT], F32)
        m_after = A.alloc("m_after", [4, NT], F32)
        m_st = A.alloc("m_st", [4, NT], F32)
        mmax = A.alloc("mmax", [4, NT], F32)
        dtmp = A.alloc("dtmp", [4, NT], F32)
        dma(gb[:], gb_d, writes=["gb"])
        op("pool", lambda e: e.memset(lns_t[:], LNS), writes=["lns"])
        op("pool", lambda e: e.memset(ones4[:], 1.0), writes=["ones4"])
        op("pool", lambda e: e.tensor_scalar(out=ngb[:], in0=gb[:], scalar1=-1.0, scalar2=None, op0=ALU.mult),
           reads=["gb"], writes=["ngb"])
        wgb, wgk = wgs.load(wg_d)

        def gate_proj(col0, Gd, gkey):
            for (ta, o, w) in tok_tiles(0, TOK):
                pi = next_ps()
                for kc in range(8):
                    op("pe", lambda e: e.matmul(ps[pi][0:4, :w], lhsT=wgb[:, kc, col0:col0 + 4], rhs=hT[:, kc, ta:ta + w],
                                                start=(kc == 0), stop=(kc == 7)),
                       reads=[wgk] + hT_keys(ta, w), writes=[PS(pi)])
                evac_copy(Gd[:, o:o + w], ps[pi][0:4, :w], [PS(pi)], [gkey])

        def v3(t):
            return t[:].rearrange("p (c t) -> p c t", t=128)

        def bc3(t):
            return t[:].unsqueeze(2).to_broadcast([4, NT, 128])

        gate_proj(4, G[0], "G0")
        dma(G[2][:], ifb_d[:, TOK:2 * TOK], writes=["G2"])
        op("dve", lambda e: e.tensor_tensor(out=G[0][:], in0=G[0][:], in1=G[2][:], op=ALU.add),
           reads=["G0", "G2"], writes=["G0"])
        op("act", lambda e: e.activation(out=G[0][:], in_=G[0][:], func=AF.Exp, scale=-1.0, bias=ngb[:, 1:2]),
           reads=["G0", "ngb"], writes=["G0"])
        op("act", lambda e: e.activation(out=G[0][:], in_=G[0][:], func=AF.Ln, scale=1.0, bias=one_t[0:4, :]),
           reads=["G0", "one"], writes=["G0"])
        for c in range(NT):
            op("dve", lambda e: e.tensor_tensor_scan(out=G[1][:, c * 128:(c + 1) * 128], data0=ones4[:],
                                                     data1=G[0][:, c * 128:(c + 1) * 128], initial=0.0,
                                                     op0=ALU.mult, op1=ALU.add),
               reads=["G0", "ones4"], writes=["G1"])
        gate_proj(0, G[0], "G0")
        dma(G[2][:], ifb_d[:, 0:TOK], reads=[], writes=["G2"])
        op("dve", lambda e: e.tensor_tensor(out=G[0][:], in0=G[0][:], in1=G[2][:], op=ALU.add),
           reads=["G0", "G2"], writes=["G0"])
        op("dve", lambda e: e.scalar_tensor_tensor(out=G[0][:], in0=G[0][:], scalar=gb[:, 0:1], in1=G[1][:],
                                                   op0=ALU.add, op1=ALU.add),
           reads=["G0", "gb", "G1"], writes=["G0"])
        op("dve", lambda e: e.tensor_reduce(out=amax[:], in_=v3(G[0]), axis=AX.X, op=ALU.max),
           reads=["G0"], writes=["amax"])
        nblast = v3(G[1])[:, :, 127]
        op("dve", lambda e: e.tensor_tensor_scan(out=m_after[:], data0=amax[:], data1=nblast, initial=0.0,
                                                 op0=ALU.max, op1=ALU.subtract),
           reads=["amax", "G1"], writes=["m_after"])
        op("pool", lambda e: e.memset(m_st[:], 0.0), writes=["m_st"])
        op("dve", lambda e: e.tensor_copy(out=m_st[:, 1:NT], in_=m_after[:, 0:NT - 1]),
           reads=["m_after", "m_st"], writes=["m_st"])
        op("dve", lambda e: e.tensor_tensor(out=mmax[:], in0=m_st[:], in1=amax[:], op=ALU.max),
           reads=["m_st", "amax"], writes=["mmax"])
        op("dve", lambda e: e.tensor_tensor(out=v3(G[2]), in0=v3(G[0]), in1=bc3(m_st), op=ALU.subtract),
           reads=["G0", "m_st"], writes=["G2"])
        op("act", lambda e: e.activation(out=G[2][:], in_=G[2][:], func=AF.Exp, scale=1.0, bias=lns_t[:]),
           reads=["G2", "lns"], writes=["G2"])
        op("dve", lambda e: e.tensor_tensor(out=v3(G[3]), in0=v3(G[1]), in1=bc3(m_st), op=ALU.subtract),
           reads=["G1", "m_st"], writes=["G3"])
        op("act", lambda e: e.activation(out=G[3][:], in_=G[3][:], func=AF.Exp),
           reads=["G3"], writes=["G3"])
        op("dve", lambda e: e.tensor_tensor(out=v3(G[4]), in0=v3(G[0]), in1=bc3(mmax), op=ALU.subtract),
           reads=["G0", "mmax"], writes=["G4"])
        op("act", lambda e: e.activation(out=G[4][:], in_=G[4][:], func=AF.Exp, scale=1.0, bias=lns_t[:]),
           reads=["G4", "lns"], writes=["G4"])
        op("dve", lambda e: e.tensor_tensor(out=dtmp[:], in0=m_st[:], in1=mmax[:], op=ALU.subtract),
           reads=["m_st", "mmax"], writes=["dtmp"])
        op("act", lambda e: e.activation(out=dtmp[:], in_=dtmp[:], func=AF.Exp), reads=["dtmp"], writes=["dtmp"])
        op("dve", lambda e: e.tensor_copy(out=v3(G[1]), in_=bc3(dtmp)), reads=["dtmp", "G3"], writes=["G1"])
        for c in range(NT):
            pi = next_ps()
            for j, gi in enumerate((2, 3, 4, 1)):
                op("pe", lambda e: e.transpose(out=ps[pi][:, j * 4:(j + 1) * 4], in_=G[gi][0:4, c * 128:(c + 1) * 128],
                                               identity=ident_f[0:4, 0:4]),
                   reads=[f"G{gi}", "ident_f"], writes=[PS(pi)])
            evac_copy(gT[:, c, :], ps[pi][:, 0:16], [PS(pi)], [("gT", c)])
        if "gT" in debug:
            d = dbg_out("gT", [128, NT * 16], F32)
            dma(d, gT[:].rearrange("p a b -> p (a b)"), reads=[("gT", c) for c in range(NT)])
        T.barrier()
        A.release(mC)
        preb = A.alloc("preb", [128, 3 + TOK], F32)
        accb = A.alloc("accb", [128, TOK], F32)
        preq = A.alloc("preq", [128, 3 + NOWN], F32)
        accq = A.alloc("accq", [128, NOWN], F32)
        kTh = A.alloc("kTh", [128, TOK], BF16)
        qTh = A.alloc("qTh", [128, NOWN], BF16)
        vaug = A.alloc("vaug", [128, NT, 130], BF16)
        osig = A.alloc("osig", [128, NQ, 128], BF16)
        Cf = A.alloc("Cf", [128, 130], F32)
        Cb2 = [A.alloc(f"Cb{i}", [128, 130], BF16) for i in range(2)]
        kwa = A.alloc("kwa", [128, NT, 128], BF16)
        Sh = [A.alloc(f"Sh{i}", [128, 128], BF16) for i in range(4)]
        hm = [A.alloc(f"hm{i}", [128, 128], F32) for i in range(4)]
        y1 = [A.alloc(f"y1{i}", [128, 128], F32) for i in range(4)]
        y2 = [A.alloc(f"y2{i}", [128, 128], BF16) for i in range(4)]
        junkc = A.alloc("junkc", [128, 128], BF16)
        sm = A.alloc("sm", [128, 2 * NT * 4], F32)
        wsC = WStage("wsC", 8, 128, nbuf=4)
        op("pool", lambda e: e.memset(preb[:, 0:3], 0.0), writes=["preb"])
        op("pool", lambda e: e.memset(preq[:, 0:3], 0.0), writes=["preq"])
        op("pool", lambda e: e.memset(vaug[:, :, 128:130], 1.0), writes=["vaug1"])

        def conv4(n, j, pre_, pkey, acc_, akey):
            N = n
            op("dve", lambda e: e.tensor_scalar(out=acc_[:, :N], in0=pre_[:, 3:3 + N], scalar1=cw[:, j, 3:4],
                                                scalar2=cw[:, j, 4:5], op0=ALU.mult, op1=ALU.add),
               reads=[pkey, "cw"], writes=[akey])
            for k in range(3):
                op("dve", lambda e: e.scalar_tensor_tensor(out=acc_[:, :N], in0=pre_[:, k:k + N], scalar=cw[:, j, k:k + 1],
                                                           in1=acc_[:, :N], op0=ALU.mult, op1=ALU.add),
                   reads=[pkey, "cw", akey], writes=[akey])

        for h in range(4):
            wb, wk_ = wsC.load(w_in[:, C_MLK + h * 128: C_MLK + (h + 1) * 128])
            proj_fm(wb, wk_, 8, 0, hT, hT_keys, 0, TOK,
                    lambda o, w, p, pk: evac_copy(preb[:, 3 + o:3 + o + w], p, [pk], ["preb"]))
            conv4(TOK, 4 + h, preb, "preb", accb, "accb")
            op("act", lambda e: e.activation(out=kTh[:], in_=accb[:], func=AF.Silu), reads=["accb"], writes=["kTh"])
            wb, wk_ = wsC.load(w_in[:, C_MLQ + h * 128: C_MLQ + (h + 1) * 128])
            def q_evac2(o, w, p, pk):
                op("act", lambda e: e.copy(out=preq[:, 3 + o:3 + o + w], in_=p), reads=[pk], writes=["preq"])
            proj_fm(wb, wk_, 8, 0, hT, hT_keys, OWN0, NOWN, q_evac2)
            conv4(NOWN, h, preq, "preq", accq, "accq")
            op("act", lambda e: e.activation(out=qTh[:], in_=accq[:], func=AF.Silu), reads=["accq"], writes=["qTh"])
            wb, wk_ = wsC.load(w_in[:, C_MLV + h * 128: C_MLV + (h + 1) * 128])
            for t in range(NT):
                pi = next_ps()
                for kc in range(8):
                    op("pe", lambda e: e.matmul(ps[pi][:, :128], lhsT=hT[:, kc, t * 128:(t + 1) * 128], rhs=wb[:, kc, :128],
                                                start=(kc == 0), stop=(kc == 7)),
                       reads=[wk_, ("hT", t)], writes=[PS(pi)])
                evac_copy(vaug[:, t, 0:128], ps[pi][:, :128], [PS(pi)], [("vaug", t)])
            wb, wk_ = wsC.load(w_in[:, C_MLO + h * 128: C_MLO + (h + 1) * 128])
            for qb in range(NQ):
                t = NP + qb
                pi = next_ps()
                for kc in range(8):
                    op("pe", lambda e: e.matmul(ps[pi][:, :128], lhsT=hT[:, kc, t * 128:(t + 1) * 128], rhs=wb[:, kc, :128],
                                                start=(kc == 0), stop=(kc == 7)),
                       reads=[wk_, ("hT", t)], writes=[PS(pi)])
                op("act", lambda e: e.activation(out=osig[:, qb, :], in_=ps[pi][:, :128], func=AF.Sigmoid),
                   reads=[PS(pi)], writes=[("osig", qb)])
            op("pool", lambda e: e.memset(Cf[:], 0.0), writes=["Cf"])
            op("pool", lambda e: e.memset(Cb2[1][:], 0.0), writes=[("Cb", 1)])
            for b0 in range(0, NT - 1, 8):
                bk = (b0 // 8) % 2
                cl = list(range(b0, min(b0 + 8, NT - 1)))
                for c in cl:
                    cs = slice(c * 128, (c + 1) * 128)
                    op("pe", lambda e: e.transpose(out=pb[bk][:, c % 8, :], in_=kTh[:, cs], identity=ident_b[:]),
                       reads=["kTh", "ident_b"], writes=[PB(bk)])
                for c in cl:
                    op("act", lambda e: e.activation(out=kwa[:, c, :], in_=pb[bk][:, c % 8, :], func=AF.Copy,
                                                     scale=gT[:, c, 8 + h:9 + h]),
                       reads=[PB(bk), "gTall"], writes=[("kwa", c)])
            pNs = {}
            for it in range(NT + 3):
                c = it
                if c < NT:
                    own = c >= NP
                    qb = c - NP
                    cs = slice(c * 128, (c + 1) * 128)
                    qs = slice(qb * 128, (qb + 1) * 128)
                    i4 = c % 4
                    if c < NT - 1:
                        pU = next_ps()
                        op("pe", lambda e: e.matmul(ps[pU][:, :129], lhsT=kwa[:, c, :], rhs=vaug[:, c, 0:129], start=True, stop=True),
                           reads=[("kwa", c), ("vaug", c), "vaug1"], writes=[PS(pU)])
                    if own:
                        pS = next_ps()
                        op("pe", lambda e: e.matmul(ps[pS][:, :128], lhsT=kTh[:, cs], rhs=qTh[:, qs], start=True, stop=True),
                           reads=["kTh", "qTh"], writes=[PS(pS)])
                        op("dve", lambda e: e.scalar_tensor_tensor(out=Sh[i4][:], in0=ps[pS][:, :128], scalar=gT[:, c, h:h + 1],
                                                                   in1=m_incl, op0=ALU.mult, op1=ALU.mult),
                           reads=[PS(pS), "gTall", "cm_f"], writes=[("Sh", i4)])
                        pN = next_ps()
                        pNs[c] = pN
                        op("pe", lambda e: e.matmul(ps[pN][:, :129], lhsT=Sh[i4][:], rhs=vaug[:, c, 0:129], start=True, stop=False),
                           reads=[("Sh", i4), ("vaug", c), "vaug1"], writes=[PS(pN)])
                        op("pe", lambda e: e.matmul(ps[pN][:, :129], lhsT=qTh[:, qs], rhs=Cb2[(c - 1) % 2][:, 0:129],
                                                    start=False, stop=True),
                           reads=["qTh", ("Cb", (c - 1) % 2)], writes=[PS(pN)])
                    if c < NT - 1:
                        op("dve", lambda e: e.scalar_tensor_tensor(out=Cb2[c % 2][:, :129], in0=Cf[:, :129],
                                                                   scalar=gT[:, c, 12 + h:13 + h], in1=ps[pU][:, :129],
                                                                   op0=ALU.mult, op1=ALU.add),
                           reads=["Cf", "gTall", PS(pU)], writes=[("Cb", c % 2)])
                        op("dve", lambda e: e.scalar_tensor_tensor(out=Cf[:, :129], in0=Cf[:, :129],
                                                                   scalar=gT[:, c, 12 + h:13 + h], in1=ps[pU][:, :129],
                                                                   op0=ALU.mult, op1=ALU.add),
                           reads=["Cf", "gTall", PS(pU)], writes=["Cf"])
                ca = it - 1
                if NP <= ca < NT:
                    c = ca
                    i4 = c % 4
                    pN = pNs[c]
                    k0 = (c * 4 + h) * 2
                    dn = sm[:, k0:k0 + 1]
                    s1 = sm[:, k0 + 1:k0 + 2]
                    op("dve", lambda e: e.tensor_scalar(out=s1, in0=ps[pN][:, 128:129], scalar1=-1.0,
                                                        scalar2=gT[:, c, 4 + h:5 + h], op0=ALU.mult, op1=ALU.max),
                       reads=[PS(pN), "gTall"], writes=[("sm1", k0)])
                    op("dve", lambda e: e.tensor_scalar(out=dn, in0=ps[pN][:, 128:129], scalar1=s1,
                                                        scalar2=None, op0=ALU.max),
                       reads=[PS(pN), ("sm1", k0)], writes=[("sm", k0)])
                    op("dve", lambda e: e.reciprocal(out=dn, in_=dn), reads=[("sm", k0)], writes=[("sm", k0)])
                    op("dve", lambda e: e.tensor_scalar(out=hm[i4][:], in0=ps[pN][:, :128], scalar1=dn, scalar2=None,
                                                        op0=ALU.mult),
                       reads=[PS(pN), ("sm", k0)], writes=[("hm", i4)])
                    op("act", lambda e: e.activation(out=junkc[:], in_=hm[i4][:], func=AF.Square, accum_out=s1),
                       reads=[("hm", i4)], writes=[("sm1", k0)])
                    op("act", lambda e: e.activation(out=s1, in_=s1, func=AF.Sqrt, scale=1.0 / 128, bias=eps_t[:]),
                       reads=[("sm1", k0), "eps"], writes=[("sm1", k0)])
                cb_ = it - 2
                if NP <= cb_ < NT:
                    c = cb_
                    i4 = c % 4
                    qb = c - NP
                    k0 = (c * 4 + h) * 2
                    s1 = sm[:, k0 + 1:k0 + 2]
                    op("dve", lambda e: e.reciprocal(out=s1, in_=s1), reads=[("sm1", k0)], writes=[("sm1", k0)])
                    op("dve", lambda e: e.scalar_tensor_tensor(out=y1[i4][:], in0=hm[i4][:], scalar=s1,
                                                               in1=gout[:, h * 128:(h + 1) * 128], op0=ALU.mult, op1=ALU.mult),
                       reads=[("hm", i4), ("sm1", k0), "gout"], writes=[("y1", i4)])
                    op("dve", lambda e: e.tensor_tensor(out=y2[i4][:], in0=y1[i4][:], in1=osig[:, qb, :], op=ALU.mult),
                       reads=[("y1", i4), ("osig", qb)], writes=[("y2", i4)])
                    op("pe", lambda e: e.transpose(out=pb[c % 2][:, 0, :], in_=y2[i4][:], identity=ident_b[:]),
                       reads=[("y2", i4), "ident_b"], writes=[PB(c % 2)])
                cc = it - 3
                if NP <= cc < NT:
                    c = cc
                    qb = c - NP
                    qs = slice(qb * 128, (qb + 1) * 128)
                    op("act", lambda e: e.copy(out=ymlT[:, h, qs], in_=pb[c % 2][:, 0, :]),
                       reads=[PB(c % 2)], writes=[("ymlT", h, qb)])
        T.barrier()
        A.release(mC0)
        if "ymlT" in debug:
            d = dbg_out("ymlT", [128, 4 * NOWN], BF16)
            dma(d, ymlT[:].rearrange("p a b -> p (a b)"), reads=[])

    if upi >= 3:
        yxaT = A.alloc("yxaT", [128, 4, NOWN], BF16)
        mD = A.mark()
        junk2 = A.alloc("junk2", [128, D], BF16)
        gmem = A.alloc("gmem", [128, D], F32)
        xag = A.alloc("xag", [128, 2], F32)
        dma(gmem[:], g_mem_bc, writes=["gmem"])
        dma(xag[:], xa_g_d, writes=["xag"])
        mnT = A.alloc("mnT", [128, 8, NMEM], BF16)
        xm = [A.alloc(f"xm{i}", [128, D], F32) for i in range(2)]
        hm_ = [A.alloc(f"hmm{i}", [128, D], BF16) for i in range(2)]
        ssD = A.alloc("ssD", [128, 2], F32)
        rsD = A.alloc("rsD", [128, 2], F32)
        rstdD = A.alloc("rstdD", [128, 2], F32)
        for mt in range(2):
            dma(xm[mt][:], mem[mt * 128:(mt + 1) * 128, :], writes=[("xm", mt)])
            norm_block(xm[mt], ("xm", mt), hm_[mt], ("hmm", mt), mt, gmem, "gmem", ssD, rsD, rstdD, "D")
            for kc in range(8):
                op("pe", lambda e: e.transpose(out=pb[mt][:, kc, :], in_=hm_[mt][:, kc * 128:(kc + 1) * 128], identity=ident_b[:]),
                   reads=[("hmm", mt), "ident_b"], writes=[PB(mt)])
            evac_copy(mnT[:, :, mt * 128:(mt + 1) * 128], pb[mt][:], [PB(mt)], ["mnT"])
        wsD = WStage("wsD", 8, 512, nbuf=2)
        khT = A.alloc("khT", [128, 4, NMEM], BF16)
        vX = A.alloc("vX", [128, 2, 512], BF16)
        raw = A.alloc("raw", [128, 512], F32)
        sq = A.alloc("sq", [128, 512], BF16)
        rsb = A.alloc("rsb", [128, 512], F32)
        qn = A.alloc("qn", [128, 512], BF16)
        Pm = A.alloc("Pm", [128, 2, 512], BF16)
        rden = A.alloc("rden", [128, 512], F32)

        def qknorm(pi, w, gcol, out_ap, out_key):
            op("act", lambda e: e.copy(out=raw[:, :w], in_=ps[pi][:, :w]), reads=[PS(pi)], writes=["raw"])
            op("act", lambda e: e.activation(out=sq[:, :w], in_=ps[pi][:, :w], func=AF.Square), reads=[PS(pi)], writes=["sq"])
            pj = next_ps()
            op("pe", lambda e: e.matmul(ps[pj][:, :w], lhsT=ones_b[:], rhs=sq[:, :w], start=True, stop=True),
               reads=["ones_b", "sq"], writes=[PS(pj)])
            op("act", lambda e: e.activation(out=rsb[:, :w], in_=ps[pj][:, :w], func=AF.Sqrt, scale=1.0 / 128, bias=eps_t[:]),
               reads=[PS(pj), "eps"], writes=["rsb"])
            op("dve", lambda e: e.reciprocal(out=rsb[:, :w], in_=rsb[:, :w]), reads=["rsb"], writes=["rsb"])
            op("dve", lambda e: e.scalar_tensor_tensor(out=out_ap, in0=raw[:, :w], scalar=xag[:, gcol:gcol + 1], in1=rsb[:, :w],
                                                       op0=ALU.mult, op1=ALU.mult),
               reads=["raw", "xag", "rsb"], writes=[out_key])

        wb, wk_ = wsD.load(w_kv[:, 0:512])
        for h in range(4):
            pi = next_ps()
            for kc in range(8):
                op("pe", lambda e: e.matmul(ps[pi][:, :NMEM], lhsT=wb[:, kc, h * 128:(h + 1) * 128], rhs=mnT[:, kc, :],
                                            start=(kc == 0), stop=(kc == 7)),
                   reads=[wk_, "mnT"], writes=[PS(pi)])
            qknorm(pi, NMEM, 1, khT[:, h, :], ("khT", h))
        wb, wk_ = wsD.load(w_kv[:, 512:1024])
        for mt in range(2):
            pi = next_ps()
            for kc in range(8):
                op("pe", lambda e: e.matmul(ps[pi][:, :512], lhsT=mnT[:, kc, mt * 128:(mt + 1) * 128], rhs=wb[:, kc, :512],
                                            start=(kc == 0), stop=(kc == 7)),
                   reads=[wk_, "mnT"], writes=[PS(pi)])
            evac_copy(vX[:, mt, :], ps[pi][:, :512], [PS(pi)], [("vX", mt)])
        wsDq = WStage("wsDq", 8, 128, nbuf=3)
        SC = float(1.0 / np.sqrt(128.0))
        for h in range(4):
            wb, wk_ = wsDq.load(w_in[:, C_XAQ + h * 128: C_XAQ + (h + 1) * 128])
            for (ta, o, w) in tok_tiles(OWN0, NOWN):
                pi = next_ps()
                for kc in range(8):
                    op("pe", lambda e: e.matmul(ps[pi][:, :w], lhsT=wb[:, kc, :128], rhs=hT[:, kc, ta:ta + w],
                                                start=(kc == 0), stop=(kc == 7)),
                       reads=[wk_] + hT_keys(ta, w), writes=[PS(pi)])
                qknorm(pi, w, 0, qn[:, :w], "qn")
                po = next_ps()
                pd = next_ps()
                for mt in range(2):
                    pz = next_ps()
                    op("pe", lambda e: e.matmul(ps[pz][:, :w], lhsT=khT[:, h, mt * 128:(mt + 1) * 128], rhs=qn[:, :w],
                                                start=True, stop=True),
                       reads=[("khT", h), "qn"], writes=[PS(pz)])
                    op("act", lambda e: e.activation(out=Pm[:, mt, :w], in_=ps[pz][:, :w], func=AF.Exp, scale=SC),
                       reads=[PS(pz)], writes=[("Pm", mt)])
                for mt in range(2):
                    op("pe", lambda e: e.matmul(ps[po][:, :w], lhsT=vX[:, mt, h * 128:(h + 1) * 128], rhs=Pm[:, mt, :w],
                                                start=(mt == 0), stop=(mt == 1)),
                       reads=[("vX", mt), ("Pm", mt)], writes=[PS(po)])
                for mt in range(2):
                    op("pe", lambda e: e.matmul(ps[pd][:, :w], lhsT=ones_b[:], rhs=Pm[:, mt, :w],
                                                start=(mt == 0), stop=(mt == 1)),
                       reads=["ones_b", ("Pm", mt)], writes=[PS(pd)])
                op("dve", lambda e: e.reciprocal(out=rden[:, :w], in_=ps[pd][:, :w]), reads=[PS(pd)], writes=["rden"])
                op("dve", lambda e: e.tensor_tensor(out=yxaT[:, h, o:o + w], in0=ps[po][:, :w], in1=rden[:, :w], op=ALU.mult),
                   reads=[PS(po), "rden"], writes=[("yxaT", h, o)])
        T.barrier()
        A.release(mD)
        if "yxaT" in debug:
            d = dbg_out("yxaT", [128, 4 * NOWN], BF16)
            dma(d, yxaT[:].rearrange("p a b -> p (a b)"), reads=[])

    if upi >= 4:
        mergedT = A.alloc_top("mergedT", [128, 8, NOWN], BF16)
        mE = A.mark()
        wsG = WStage("wsEg", 8, 128, nbuf=4)
        wsO = WStage("wsEo", 4, 128, nbuf=4)
        gsb = [A.alloc(f"gsb{i}", [128, 512], F32) for i in range(2)]
        tmpE = [A.alloc(f"tmpE{i}", [128, 512], F32) for i in range(2)]
        accE = A.alloc("accE", [128, NOWN], F32)
        ybr = [ysbT, ymlT, yxaT]
        u = 0
        for ct in range(8):
            for br in range(3):
                c0 = C_GATE + br * 1024 + ct * 128
                wg, wgk_ = wsG.load(w_in[:, c0:c0 + 128])
                wo_, wok_ = wsO.load(wbr_d[br][:, ct * 128:(ct + 1) * 128])
                for (ta, o, w) in tok_tiles(0, NOWN):
                    i2 = u % 2
                    u += 1
                    pg = next_ps()
                    for kc in range(8):
                        op("pe", lambda e: e.matmul(ps[pg][:, :w], lhsT=wg[:, kc, :128], rhs=hT[:, kc, OWN0 + o:OWN0 + o + w],
                                                    start=(kc == 0), stop=(kc == 7)),
                           reads=[wgk_], writes=[PS(pg)])
                    pp = next_ps()
                    for kc in range(4):
                        op("pe", lambda e: e.matmul(ps[pp][:, :w], lhsT=wo_[:, kc, :128], rhs=ybr[br][:, kc, o:o + w],
                                                    start=(kc == 0), stop=(kc == 3)),
                           reads=[wok_], writes=[PS(pp)])
                    op("act", lambda e: e.activation(out=gsb[i2][:, :w], in_=ps[pg][:, :w], func=AF.Sigmoid),
                       reads=[PS(pg)], writes=[("gsb", i2)])
                    if br == 0:
                        op("dve", lambda e: e.tensor_tensor(out=accE[:, o:o + w], in0=ps[pp][:, :w], in1=gsb[i2][:, :w], op=ALU.mult),
                           reads=[PS(pp), ("gsb", i2)], writes=[("accE", o)])
                    else:
                        op("dve", lambda e: e.tensor_tensor(out=tmpE[i2][:, :w], in0=ps[pp][:, :w], in1=gsb[i2][:, :w], op=ALU.mult),
                           reads=[PS(pp), ("gsb", i2)], writes=[("tmpE", i2)])
                        if br == 1:
                            op("dve", lambda e: e.tensor_tensor(out=accE[:, o:o + w], in0=accE[:, o:o + w], in1=tmpE[i2][:, :w], op=ALU.add),
                               reads=[("accE", o), ("tmpE", i2)], writes=[("accE", o)])
                        else:
                            op("dve", lambda e: e.tensor_tensor(out=mergedT[:, ct, o:o + w], in0=accE[:, o:o + w], in1=tmpE[i2][:, :w], op=ALU.add),
                               reads=[("accE", o), ("tmpE", i2)], writes=[("mergedT", ct, o)])
        T.barrier()
        if "mergedT" in debug:
            d = dbg_out("mergedT", [128, 8 * NOWN], BF16)
            dma(d, mergedT[:].rearrange("p a b -> p (a b)"), reads=[])
        A.release(mConst)
        h2T = A.alloc("h2T", [128, 8, NOWN], BF16)
        mE2 = A.mark()
        junk2 = A.alloc("junk2", [128, D], BF16)
        wo_b = A.alloc("wo_b", [128, 8, D], BF16)
        gffn = A.alloc("gffn", [128, D], F32)
        valid = A.alloc("valid", [128, NQ], F32)
        dma(gffn[:], g_ffn_bc, writes=["gffn"])
        dma(valid[:], valid_d, writes=["valid"])
        for half in range(2):
            dma(wo_b[:, :, half * 512:(half + 1) * 512], w_o_d[:, half * 512:(half + 1) * 512].rearrange("(kc p) n -> p kc n", p=128),
                writes=["wo_b"], q="pool")
        xe = [A.alloc(f"xe{i}", [128, D], F32) for i in range(3)]
        x1t = [A.alloc(f"x1t{i}", [128, D], F32) for i in range(3)]
        hbe = [A.alloc(f"hbe{i}", [128, D], BF16) for i in range(3)]
        ssE = A.alloc("ssE", [128, NQ], F32)
        rsE = A.alloc("rsE", [128, NQ], F32)
        rstdE = A.alloc("rstdE", [128, NQ], F32)
        def e2_front(qb):
            i2 = qb % 3
            dma(xe[i2][:], xs[OWN0 + qb * 128:OWN0 + (qb + 1) * 128, :], writes=[("xe", i2)])
            for half in range(2):
                pi = next_ps()
                for kc in range(8):
                    op("pe", lambda e: e.matmul(ps[pi][:, :512], lhsT=mergedT[:, kc, qb * 128:(qb + 1) * 128],
                                                rhs=wo_b[:, kc, half * 512:(half + 1) * 512], start=(kc == 0), stop=(kc == 7)),
                       reads=["wo_b"], writes=[PS(pi)])
                op("dve", lambda e: e.tensor_tensor(out=x1t[i2][:, half * 512:(half + 1) * 512], in0=ps[pi][:, :512],
                                                    in1=xe[i2][:, half * 512:(half + 1) * 512], op=ALU.add),
                   reads=[PS(pi), ("xe", i2)], writes=[("x1t", i2)])
            dma(x1s[qb * 128:(qb + 1) * 128, :], x1t[i2][:], reads=[("x1t", i2)], writes=[("x1s", qb)])
            op("act", lambda e: e.activation(out=junk2[:], in_=x1t[i2][:], func=AF.Square, accum_out=ssE[:, qb:qb + 1]),
               reads=[("x1t", i2)], writes=[("E", "ss", qb)])
            op("act", lambda e: e.activation(out=rsE[:, qb:qb + 1], in_=ssE[:, qb:qb + 1], func=AF.Sqrt,
                                             scale=1.0 / D, bias=eps_t[:]),
               reads=[("E", "ss", qb), "eps"], writes=[("E", "rs", qb)])

        def e2_back(qb):
            i2 = qb % 3
            ib = qb % 2
            op("dve", lambda e: e.reciprocal(out=rstdE[:, qb:qb + 1], in_=rsE[:, qb:qb + 1]),
               reads=[("E", "rs", qb)], writes=[("E", "rstd", qb)])
            op("dve", lambda e: e.tensor_tensor(out=rstdE[:, qb:qb + 1], in0=rstdE[:, qb:qb + 1], in1=valid[:, qb:qb + 1], op=ALU.mult),
               reads=[("E", "rstd", qb), "valid"], writes=[("E", "rstd", qb)])
            op("dve", lambda e: e.scalar_tensor_tensor(out=hbe[i2][:], in0=x1t[i2][:], scalar=rstdE[:, qb:qb + 1], in1=gffn[:],
                                                       op0=ALU.mult, op1=ALU.mult),
               reads=[("x1t", i2), ("E", "rstd", qb), "gffn"], writes=[("hbe", i2)])
            for kc in range(8):
                op("pe", lambda e: e.transpose(out=pb[ib][:, kc, :], in_=hbe[i2][:, kc * 128:(kc + 1) * 128], identity=ident_b[:]),
                   reads=[("hbe", i2), "ident_b"], writes=[PB(ib)])

        def e2_evac(qb):
            ib = qb % 2
            evac_copy(h2T[:, :, qb * 128:(qb + 1) * 128], pb[ib][:], [PB(ib)], [("h2T", qb)])

        for qb in range(NQ + 2):
            if qb < NQ:
                e2_front(qb)
            if 1 <= qb <= NQ:
                e2_back(qb - 1)
            if qb >= 2:
                e2_evac(qb - 2)
        T.barrier()
        A.release(mE2)
        A.hi = A.hi0
        if "x1" in debug:
            d = dbg_out("x1", [NOWN, D], F32)
            dma(d, x1s, reads=[("x1s", qb) for qb in range(NQ)])

    if upi >= 5:
        NF = 16 * 128
        aT = A.alloc_top("aT", [128, 22, NF], BF16)
        mF = A.mark()
        upv = A.alloc("upv", [128, 2 + NOWN], F32)
        upg = A.alloc("upg", [128, 2 + NOWN], F32)
        acv = [A.alloc(f"acv{i}", [128, NOWN], F32) for i in range(2)]
        acg = [A.alloc(f"acg{i}", [128, NOWN], F32) for i in range(2)]
        cf = A.alloc("cf", [128, 44, 4], F32)
        dma(cf[:].rearrange("p a b -> p (a b)"), cf_d, writes=["cf"])
        wsU = WStage("wsU", 8, 256, nbuf=4)
        op("pool", lambda e: e.memset(upv[:, 0:2], 0.0), writes=["upv"])
        op("pool", lambda e: e.memset(upg[:, 0:2], 0.0), writes=["upg"])

        def h2_keys(ta, w):
            return []

        def act_evac(dst, dkey):
            def f(o, w, p, pk):
                op("act", lambda e: e.copy(out=dst[:, 2 + o:2 + o + w], in_=p), reads=[pk], writes=[dkey])
            return f

        def conv3(src_, skey, dst, dkey, j):
            op("act", lambda e: e.activation(out=dst[:], in_=src_[:, 2:2 + NOWN], func=AF.Identity,
                                             scale=cf[:, j, 2:3], bias=cf[:, j, 3:4]),
               reads=[skey, "cf"], writes=[dkey])
            for k in range(2):
                op("dve", lambda e: e.scalar_tensor_tensor(out=dst[:], in0=src_[:, k:k + NOWN], scalar=cf[:, j, k:k + 1],
                                                           in1=dst[:], op0=ALU.mult, op1=ALU.add),
                   reads=[skey, "cf", dkey], writes=[dkey])

        def tail(j):
            jb = j % 2
            op("act", lambda e: e.activation(out=acg[jb][:], in_=acg[jb][:], func=AF.Silu),
               reads=[("acg", jb)], writes=[("acg", jb)])
            op("dve", lambda e: e.tensor_tensor(out=aT[:, j, :], in0=acg[jb][:, 128:], in1=acv[jb][:, 128:], op=ALU.mult),
               reads=[("acg", jb), ("acv", jb)], writes=[("aT", j)])

        for j in range(22):
            jb = j % 2
            if j % 2 == 0:
                wv, wvk = wsU.load(w_up[:, j * 128:(j + 2) * 128])
                wg, wgk_ = wsU.load(w_up[:, DFF + j * 128:DFF + (j + 2) * 128])
            proj_fm(wv, wvk, 8, (j % 2) * 128, h2T, h2_keys, 0, NOWN, act_evac(upv, "upv"))
            proj_fm(wg, wgk_, 8, (j % 2) * 128, h2T, h2_keys, 0, NOWN, act_evac(upg, "upg"))
            conv3(upv, "upv", acv[jb], ("acv", jb), j)
            conv3(upg, "upg", acg[jb], ("acg", jb), 22 + j)
            if j > 0:
                tail(j - 1)
        tail(21)
        T.barrier()
        A.release(mConst)
        wd_b = A.alloc("wd_b", [128, 22, D], BF16)
        for half in range(2):
            for jj in range(2):
                dma(wd_b[:, jj * 11:(jj + 1) * 11, half * 512:(half + 1) * 512],
                    w_down[jj * 11 * 128:(jj + 1) * 11 * 128, half * 512:(half + 1) * 512].rearrange("(j p) n -> p j n", p=128),
                    writes=["wd_b"], q="pool")
        x1r = [A.alloc(f"x1r{i}", [128, D], F32) for i in range(2)]
        ob = [A.alloc(f"ob{i}", [128, D], F32) for i in range(2)]
        for qb in range(1, NQ):
            i2 = qb % 2
            dma(x1r[i2][:], x1s[qb * 128:(qb + 1) * 128, :], reads=[("x1s", qb)], writes=[("x1r", i2)])
            for half in range(2):
                pi = next_ps()
                for j in range(22):
                    op("pe", lambda e: e.matmul(ps[pi][:, :512], lhsT=aT[:, j, (qb - 1) * 128:qb * 128],
                                                rhs=wd_b[:, j, half * 512:(half + 1) * 512], start=(j == 0), stop=(j == 21)),
                       reads=["wd_b"], writes=[PS(pi)])
                op("dve", lambda e: e.tensor_tensor(out=ob[i2][:, half * 512:(half + 1) * 512], in0=ps[pi][:, :512],
                                                    in1=x1r[i2][:, half * 512:(half + 1) * 512], op=ALU.add),
                   reads=[PS(pi), ("x1r", i2)], writes=[("ob", i2)])
            dma(out_d[(qb - 1) * 128:qb * 128, :], ob[i2][:], reads=[("ob", i2)], writes=[("out", qb)])

    T.finish()
    nc._nw = T.nwait
    return nc, dbg


def _prep_inputs(inputs):
    f = lambda k: np.asarray(inputs[k], np.float32)
    x = f("x")
    maps = []
    ident = np.eye(128, dtype=np.float32)
    ii = np.arange(128)
    strict = (ii[:, None] < ii[None, :]).astype(np.float32)
    incl = (ii[:, None] <= ii[None, :]).astype(np.float32)
    tm = (ii[:, None] >= ii[None, :]).astype(np.float32)
    um = (ii[:, None] < ii[None, :]).astype(np.float32)
    rowmask = np.stack([(ii < 64), (ii >= 64)], axis=1).astype(np.float32)
    cmasks = np.ascontiguousarray(np.concatenate([strict, incl, tm, um], axis=1))
    bc = lambda v: np.ascontiguousarray(np.broadcast_to(np.asarray(v, np.float32).reshape(1, -1), (128, v.size)))
    cwv = f("ml_conv_w")[0]
    cbv = f("ml_conv_b")[0]
    cw = np.zeros((128, 8, 5), np.float32)
    for j in range(8):
        cols = j * 128 + ii
        cw[:, j, 0:4] = cwv[:, cols].T
        cw[:, j, 4] = cbv[cols]
    fw = f("ff_conv_w")[0]
    fb = f("ff_conv_b")[0]
    cf = np.zeros((128, 44, 4), np.float32)
    for j in range(44):
        cols = j * 128 + ii
        cf[:, j, 0:3] = fw[:, cols].T
        cf[:, j, 3] = fb[cols]
    gb = np.stack([f("ml_b_i")[0], f("ml_b_f")[0]], axis=1)
    xa_g = np.stack([f("xa_g_q")[0], f("xa_g_k")[0]], axis=1)
    shared = {
        "w_in": np.ascontiguousarray(f("w_in")[0]),
        "g_mix_bc": bc(f("g_mix")[0]),
        "ident": ident, "cmasks": cmasks, "rowmask": rowmask,
        "g_mem_bc": bc(f("g_mem")[0]),
        "xa_g": np.ascontiguousarray(xa_g),
        "w_mem_kv": np.ascontiguousarray(f("w_mem_kv")[0]),
        "gb": np.ascontiguousarray(gb),
        "cw": np.ascontiguousarray(cw.reshape(128, 40)),
        "gout_bc": bc(f("ml_g_out")[0].reshape(-1)),
        "w_sb_out": np.ascontiguousarray(f("w_sb_out")[0]),
        "w_ml_out": np.ascontiguousarray(f("w_ml_out")[0]),
        "w_xa_out": np.ascontiguousarray(f("w_xa_out")[0]),
        "w_o": np.ascontiguousarray(f("w_o")[0]),
        "g_ffn_bc": bc(f("g_ffn")[0]),
        "w_up": np.ascontiguousarray(f("w_up")[0]),
        "cf": np.ascontiguousarray(cf.reshape(128, 176)),
        "w_down": np.ascontiguousarray(f("w_down")[0]),
    }
    for c in range(8):
        b, p = c // 2, c % 2
        kbias = np.zeros((128, NT), np.float32)
        ifb = np.zeros((4, 2 * TOK), np.float32)
        valid = np.ones((128, NQ), np.float32)
        if p == 1:
            xs = x[b]
        else:
            xs = np.concatenate([np.zeros((2048, D), np.float32), x[b, :2048]], axis=0)
            kbias[:, :16] = NEG
            ifb[:, 0:2048] = NEG
            ifb[:, TOK:TOK + 2048] = -NEG
            valid[:, 0] = 0.0
        m = dict(shared)
        m.update({"xs": np.ascontiguousarray(xs), "mem": np.ascontiguousarray(f("mem")[b]),
                  "kbias": kbias, "ifbias": ifb, "valid": valid})
        maps.append(m)
    return maps


def kernel(**inputs):
    inputs = {k: np.asarray(v) for k, v in inputs.items()}
    nc, _ = build_program()
    maps = _prep_inputs(inputs)
    res = run_bass_kernel_spmd(nc, maps, core_ids=list(range(8)))
    out = np.zeros((4, SEQ, D), np.float32)
    for c in range(8):
        b, p = c // 2, c % 2
        out[b, p * 2048:(p + 1) * 2048] = res.results[c]["out"]
    return out
```

```python
import numpy as np
import concourse.bass as bass
import concourse.mybir as mybir
from concourse.bass_utils import run_bass_kernel_spmd

F32 = mybir.dt.float32
BF16 = mybir.dt.bfloat16
AF = mybir.ActivationFunctionType
ALU = mybir.AluOpType
AX = mybir.AxisListType

D = 1024
SEQ = 4096
NT = 32
NP = 15
NQ = 17
TOK = NT * 128
OWN0 = NP * 128
NOWN = NQ * 128
DFF = 2816
NMEM = 256
EPS = 1e-6
NEG = -30000.0

C_SBQ, C_SBK, C_SBV = 0, 512, 1024
C_MLQ, C_MLK, C_MLV, C_MLO = 1536, 2048, 2560, 3072
C_MLI, C_MLF = 3584, 3588
C_XAQ = 3592
C_GATE = 4104

SAME_ENGINE_RAW = True
STRICT_SYNC = False


class Buf:
    __slots__ = ("name", "w", "r")

    def __init__(self, name=""):
        self.name = name
        self.w = None
        self.r = {}


class Tracker:
    ND = 6

    def __init__(self, nc):
        self.nc = nc
        self.E = dict(pe=nc.tensor, act=nc.scalar, dve=nc.vector, pool=nc.gpsimd, sp=nc.sync)
        self.sem = {k: nc.alloc_semaphore("s_" + k) for k in ("pe", "act", "dve", "pool")}
        self.cnt = {k: 0 for k in self.sem}
        self.seen = {k: {} for k in self.E}
        self.dsem = {}
        self.dcnt = {}
        self.dnext = {}
        for q in ("sp", "pool", "act"):
            self.dsem[q] = [nc.alloc_semaphore(f"d_{q}{i}") for i in range(self.ND)]
            self.dcnt[q] = [0] * self.ND
            self.dnext[q] = 0
        self.bufs = {}
        self.nwait = 0

    def buf(self, key):
        b = self.bufs.get(key)
        if b is None:
            b = self.bufs[key] = Buf(str(key))
        return b

    def _wait(self, eng, tok):
        sem, val, key = tok
        if self.seen[eng].get(key, 0) >= val:
            return
        self.E[eng].wait_ge(sem, val)
        self.seen[eng][key] = val
        self.nwait += 1

    def _deps(self, eng, reads, writes):
        for b in reads:
            if b.w is not None:
                if b.w[2] == eng and (eng == "pe" or not SAME_ENGINE_RAW):
                    continue
                self._wait(eng, b.w)
        for b in writes:
            if b.w is not None and (b.w[2] != eng or (STRICT_SYNC and eng != "pe")):
                self._wait(eng, b.w)
            for k, tok in b.r.items():
                if k != eng or (STRICT_SYNC and eng != "pe"):
                    self._wait(eng, tok)

    def _mark(self, tok, reads, writes):
        for b in reads:
            b.r[tok[2]] = tok
        for b in writes:
            b.w = tok
            b.r = {}

    def op(self, eng, fn, reads=(), writes=()):
        reads = [self.buf(b) if not isinstance(b, Buf) else b for b in reads]
        writes = [self.buf(b) if not isinstance(b, Buf) else b for b in writes]
        self._deps(eng, reads, writes)
        ins = fn(self.E[eng])
        self.cnt[eng] += 1
        ins.then_inc(self.sem[eng], 1)
        tok = (self.sem[eng], self.cnt[eng], eng)
        self._mark(tok, reads, writes)
        return tok

    def dma(self, out, in_, reads=(), writes=(), q="sp"):
        reads = [self.buf(b) if not isinstance(b, Buf) else b for b in reads]
        writes = [self.buf(b) if not isinstance(b, Buf) else b for b in writes]
        i = self.dnext[q]
        self.dnext[q] = (i + 1) % self.ND
        key = f"d_{q}{i}"
        if self.dcnt[q][i] > 0:
            self._wait(q, (self.dsem[q][i], 16 * self.dcnt[q][i], key))
        self._deps(q, reads, writes)
        ins = self.E[q].dma_start(out=out, in_=in_)
        self.dcnt[q][i] += 1
        ins.then_inc(self.dsem[q][i], 16)
        tok = (self.dsem[q][i], 16 * self.dcnt[q][i], key)
        self._mark(tok, reads, writes)
        return tok

    def barrier(self):
        toks = [(self.sem[k], self.cnt[k], k) for k in self.sem if self.cnt[k] > 0]
        for q in self.dsem:
            for i in range(self.ND):
                if self.dcnt[q][i] > 0:
                    toks.append((self.dsem[q][i], 16 * self.dcnt[q][i], f"d_{q}{i}"))
        for eng in self.E:
            for tok in toks:
                if tok[2] == eng:
                    continue
                self._wait(eng, tok)

    def finish(self):
        toks = []
        for q in self.dsem:
            for i in range(self.ND):
                if self.dcnt[q][i] > 0:
                    toks.append((self.dsem[q][i], 16 * self.dcnt[q][i], f"d_{q}{i}"))
        for tok in toks:
            self._wait("sp", tok)
        self.E["sp"].nop()


class Arena:
    def __init__(self, nc, lo=16512, hi=229344):
        self.nc = nc
        self.lo = lo
        self.hi = hi
        self.hi0 = hi
        self.top = lo
        self.n = 0

    def alloc(self, name, shape, dtype):
        per = 1
        for s in shape[1:]:
            per *= s
        nbytes = per * (4 if dtype == F32 else 2)
        off = (self.top + 63) // 64 * 64
        assert off + nbytes <= self.hi, f"SBUF overflow allocating {name}: {off}+{nbytes} > {self.hi}"
        self.top = off + nbytes
        self.n += 1
        return self.nc.alloc_sbuf_tensor_at(f"{name}_{self.n}", list(shape), dtype, offset=off)

    def alloc_top(self, name, shape, dtype):
        per = 1
        for s in shape[1:]:
            per *= s
        nbytes = per * (4 if dtype == F32 else 2)
        off = (self.hi - nbytes) // 64 * 64
        assert off >= self.top, f"SBUF overflow (top) allocating {name}"
        self.hi = off
        self.n += 1
        return self.nc.alloc_sbuf_tensor_at(f"{name}_{self.n}", list(shape), dtype, offset=off)

    def mark(self):
        return self.top

    def release(self, m):
        self.top = m


def build_program(debug=None, upto="all"):
    debug = debug or []
    nc = bass.Bass("TRN2", target_bir_lowering=False)
    T = Tracker(nc)
    A = Arena(nc)
    op, dma = T.op, T.dma
    PH = ["A", "B", "C", "D", "E", "F", "all"]
    upi = PH.index(upto)

    def din(name, shape, dt=F32):
        return nc.dram_tensor(name, list(shape), dt, kind="ExternalInput").ap()

    xs = din("xs", [TOK, D])
    mem = din("mem", [NMEM, D])
    w_in = din("w_in", [D, 7176])
    g_mix_bc = din("g_mix_bc", [128, D])
    kbias_d = din("kbias", [128, NT])
    ident_d = din("ident", [128, 128])
    cmask_d = din("cmasks", [128, 4 * 128])
    rowmask_d = din("rowmask", [128, 2])
    g_mem_bc = din("g_mem_bc", [128, D])
    xa_g_d = din("xa_g", [128, 2])
    w_kv = din("w_mem_kv", [D, 1024])
    wg_d = w_in[:, C_MLI:C_MLI + 8]
    gb_d = din("gb", [4, 2])
    ifb_d = din("ifbias", [4, 2 * TOK])
    cw_d = din("cw", [128, 8 * 5])
    gout_d = din("gout_bc", [128, 512])
    wbr_d = [din("w_sb_out", [512, D]), din("w_ml_out", [512, D]), din("w_xa_out", [512, D])]
    w_o_d = din("w_o", [D, D])
    g_ffn_bc = din("g_ffn_bc", [128, D])
    valid_d = din("valid", [128, NQ])
    w_up = din("w_up", [D, 2 * DFF])
    cf_d = din("cf", [128, 44 * 4])
    w_down = din("w_down", [DFF, D])
    x1s = nc.dram_tensor("x1s", [NOWN, D], F32).ap()
    out_d = nc.dram_tensor("out", [16 * 128, D], F32, kind="ExternalOutput").ap()
    dbg = {}

    def dbg_out(name, shape, dt=F32):
        dbg[name] = nc.dram_tensor("dbg_" + name, list(shape), dt, kind="ExternalOutput").ap()
        return dbg[name]

    ps = [nc.alloc_psum_tensor(f"ps{i}", [128, 512], F32) for i in range(6)]
    pb = [nc.alloc_psum_tensor(f"pb{i}", [128, 8, 128], BF16) for i in range(2)]
    PS = lambda i: ("ps", i)
    PB = lambda i: ("pb", i)

    ident_f = A.alloc("ident_f", [128, 128], F32)
    ident_b = A.alloc("ident_b", [128, 128], BF16)
    cm_f = A.alloc("cm_f", [128, 512], F32)
    cm_b = A.alloc("cm_b", [128, 512], BF16)
    mask4 = A.alloc("mask4", [128, 512], F32)
    ones_b = A.alloc("ones_b", [128, 128], BF16)
    eps_t = A.alloc("eps", [128, 1], F32)
    one_t = A.alloc("one", [128, 1], F32)
    kbias = A.alloc("kbias", [128, NT], F32)
    dma(ident_f[:], ident_d, writes=["ident_f"])
    dma(cm_f[:], cmask_d, writes=["cm_f"])
    dma(kbias[:], kbias_d, writes=["kbias"])
    rowmask = A.alloc("rowmask", [128, 2], F32)
    dma(rowmask[:], rowmask_d, writes=["rowmask"])
    op("pool", lambda e: e.memset(eps_t[:], EPS), writes=["eps"])
    op("pool", lambda e: e.memset(one_t[:], 1.0), writes=["one"])
    op("pool", lambda e: e.memset(ones_b[:], 1.0), writes=["ones_b"])
    op("dve", lambda e: e.tensor_copy(out=ident_b[:], in_=ident_f[:]), reads=["ident_f"], writes=["ident_b"])
    op("dve", lambda e: e.tensor_copy(out=cm_b[:], in_=cm_f[:]), reads=["cm_f"], writes=["cm_b"])
    for r in range(4):
        op("dve", lambda e: e.tensor_copy(out=mask4[:, r * 128:(r + 1) * 128], in_=cm_f[:, 0:128]),
           reads=["cm_f"], writes=["mask4"])
    m_strict = cm_f[:, 0:128]
    m_incl = cm_f[:, 128:256]
    Tm = cm_b[:, 256:384]
    Um = cm_b[:, 384:512]

    mConst = A.mark()
    class WStage:
        def __init__(self, name, nkc, ncols, nbuf=2):
            self.name, self.nkc, self.ncols, self.nbuf = name, nkc, ncols, nbuf
            self.b = [A.alloc(f"{name}_b{i}", [128, nkc, ncols], BF16) for i in range(nbuf)]
            self.i = 0

        def load(self, src_ap, ncols=None, nkc=None):
            ncols = ncols or self.ncols
            nkc = nkc or self.nkc
            i = self.i
            self.i = (i + 1) % self.nbuf
            bk = (self.name, "b", i)
            dma(self.b[i][:, :nkc, :ncols], src_ap.rearrange("(kc p) n -> p kc n", p=128), writes=[bk], q="pool")
            return self.b[i], bk

    evac_rr = [0]

    def evac_copy(out_ap, in_ap, reads, writes):
        evac_rr[0] ^= 1
        if evac_rr[0]:
            op("act", lambda e: e.copy(out=out_ap, in_=in_ap), reads=reads, writes=writes)
        else:
            op("dve", lambda e: e.tensor_copy(out=out_ap, in_=in_ap), reads=reads, writes=writes)

    ps_rr = [0]

    def next_ps(n=6):
        ps_rr[0] = (ps_rr[0] + 1) % n
        return ps_rr[0]

    def tok_tiles(n0, n):
        o = 0
        while o < n:
            w = min(512, n - o)
            yield n0 + o, o, w
            o += w

    def proj_fm(wb, wkey, nkc, col0, inT, in_keys, tok0, ntok, evac):
        for (ta, o, w) in tok_tiles(tok0, ntok):
            pi = next_ps()
            for kc in range(nkc):
                op("pe", lambda e: e.matmul(ps[pi][:, :w], lhsT=wb[:, kc, col0:col0 + 128], rhs=inT[:, kc, ta:ta + w],
                                            start=(kc == 0), stop=(kc == nkc - 1)),
                   reads=[wkey] + in_keys(ta, w), writes=[PS(pi)])
            evac(o, w, ps[pi][:, :w], PS(pi))

    def hT_keys(ta, w):
        return [("hT", t) for t in range(ta // 128, (ta + w - 1) // 128 + 1)]

    hT = A.alloc("hT", [128, 8, TOK], BF16)
    mA = A.mark()
    gmix = A.alloc("gmix", [128, D], F32)
    dma(gmix[:], g_mix_bc, writes=["gmix"])
    xin = [A.alloc(f"xin{i}", [128, D], F32) for i in range(4)]
    hb = [A.alloc(f"hb{i}", [128, D], BF16) for i in range(4)]
    junk = A.alloc("junk", [128, D], BF16)
    ss = A.alloc("ss", [128, NT], F32)
    rs = A.alloc("rs", [128, NT], F32)
    rstd = A.alloc("rstd", [128, NT], F32)
    def a_front(t):
        i = t % 4
        X, H, P = xin[i], hb[i], pb[t % 2]
        dma(X[:], xs[t * 128:(t + 1) * 128, :], writes=[f"xin{i}"])
        op("act", lambda e: e.activation(out=junk[:], in_=X[:], func=AF.Square, accum_out=ss[:, t:t + 1]),
           reads=[f"xin{i}"], writes=[("ss", t)])
        op("act", lambda e: e.activation(out=rs[:, t:t + 1], in_=ss[:, t:t + 1], func=AF.Sqrt,
                                         scale=1.0 / D, bias=eps_t[:]),
           reads=[("ss", t), "eps"], writes=[("rs", t)])
        op("dve", lambda e: e.reciprocal(out=rstd[:, t:t + 1], in_=rs[:, t:t + 1]),
           reads=[("rs", t)], writes=[("rstd", t)])
        op("dve", lambda e: e.scalar_tensor_tensor(out=H[:], in0=X[:], scalar=rstd[:, t:t + 1], in1=gmix[:],
                                                   op0=ALU.mult, op1=ALU.mult),
           reads=[f"xin{i}", ("rstd", t), "gmix"], writes=[f"hb{i}"])
        for kc in range(8):
            op("pe", lambda e: e.transpose(out=P[:, kc, :], in_=H[:, kc * 128:(kc + 1) * 128], identity=ident_b[:]),
               reads=[f"hb{i}", "ident_b"], writes=[PB(t % 2)])

    def a_back(t):
        evac_copy(hT[:, :, t * 128:(t + 1) * 128], pb[t % 2][:], [PB(t % 2)], [("hT", t)])

    for t in range(NT + 1):
        if t < NT:
            a_front(t)
        if t >= 1:
            a_back(t - 1)
    T.barrier()
    A.release(mA)

    if "hT" in debug:
        d = dbg_out("hT", [128, 8 * TOK], BF16)
        dma(d, hT[:].rearrange("p a b -> p (a b)"), reads=[("hT", t) for t in range(NT)])

    ysbT = A.alloc("ysbT", [128, 4, NOWN], BF16)

    if upi >= 1:
        mB = A.mark()
        ws = WStage("wsB", 8, 256, nbuf=2)
        kT = A.alloc("kT", [128, 2, TOK], BF16)
        qT = A.alloc("qT", [128, 4, NOWN], BF16)
        vS = A.alloc("vS", [128, NT, 256], BF16)
        Et = [[A.alloc(f"E{s}{i}", [128, 512], F32) for i in range(2)] for s in range(2)]
        Lt = [[A.alloc(f"L{s}{i}", [128, 512], BF16) for i in range(2)] for s in range(2)]
        Xt = [[A.alloc(f"X{s}{i}", [128, 512], F32) for i in range(2)] for s in range(2)]
        Wt = [[A.alloc(f"W{s}{i}", [128, 512], BF16) for i in range(2)] for s in range(2)]
        for hg in range(2):
            wb, wk = ws.load(w_in[:, C_SBK + hg * 256: C_SBK + (hg + 1) * 256])
            for pr in range(2):
                proj_fm(wb, wk, 8, pr * 128, hT, hT_keys, 0, TOK,
                        lambda o, w, p, pk, pr=pr: evac_copy(kT[:, pr, o:o + w], p, [pk],
                                                             [("kT", pr, t) for t in range(o // 128, (o + w) // 128)]))
            wb, wk = ws.load(w_in[:, C_SBQ + hg * 256: C_SBQ + (hg + 1) * 256])
            for pr in range(2):
                def q_evac(o, w, p, pk, pr=pr):
                    wr = [("qT", pr, t) for t in range(o // 128, (o + w) // 128)]
                    op("act", lambda e: e.activation(out=qT[:, pr, o:o + w], in_=p, func=AF.Copy, scale=rowmask[:, 0:1]),
                       reads=[pk, "rowmask"], writes=wr)
                    op("dve", lambda e: e.tensor_scalar(out=qT[:, 2 + pr, o:o + w], in0=p, scalar1=rowmask[:, 1:2], scalar2=None,
                                                        op0=ALU.mult),
                       reads=[pk, "rowmask"], writes=wr)
                proj_fm(wb, wk, 8, pr * 128, hT, hT_keys, OWN0, NOWN, q_evac)
            wb, wk = ws.load(w_in[:, C_SBV + hg * 256: C_SBV + (hg + 1) * 256])
            for t in range(NT):
                pi = next_ps()
                for kc in range(8):
                    op("pe", lambda e: e.matmul(ps[pi][:, :256], lhsT=hT[:, kc, t * 128:(t + 1) * 128], rhs=wb[:, kc, :256],
                                                start=(kc == 0), stop=(kc == 7)),
                       reads=[wk, ("hT", t)], writes=[PS(pi)])
                evac_copy(vS[:, t, :], ps[pi][:, :256], [PS(pi)], [("vS", t)])
            NCH = 2
            order = list(range(NQ - 1, -1, -1))

            def emit_z(s, qb, kb):
                for h4 in range(4):
                    pr, half = h4 // 2, h4 % 2
                    op("pe", lambda e: e.matmul(ps[s][:, h4 * 128:(h4 + 1) * 128],
                                                lhsT=kT[:, pr, kb * 128:(kb + 1) * 128],
                                                rhs=qT[:, half * 2 + pr, qb * 128:(qb + 1) * 128],
                                                start=(h4 == 0), stop=True, skip_group_check=True),
                       reads=[("kT", pr, kb), ("qT", pr, qb)], writes=[PS(s)])

            cur = [None] * NCH
            for s in range(NCH):
                if order:
                    qb = order.pop(0)
                    cur[s] = (qb, NP + qb)
                    emit_z(s, *cur[s])
            step = 0
            while any(c is not None for c in cur):
                act_ = [s for s in range(NCH) if cur[s] is not None]
                bi = step % 2
                step += 1
                info = {}
                nxt = [None] * NCH
                for s in act_:
                    qb, kb = cur[s]
                    gq = NP + qb
                    info[s] = (qb, kb, gq, kb == gq, kb == 0)
                    if kb > 0:
                        nxt[s] = (qb, kb - 1)
                    elif order:
                        q2 = order.pop(0)
                        nxt[s] = (q2, NP + q2)
                for s in act_:
                    qb, kb, gq, first, last = info[s]
                    E_ = Et[s][bi]
                    op("act", lambda e: e.activation(out=E_[:], in_=ps[s][:], func=AF.Exp, scale=0.125,
                                                     bias=kbias[:, kb:kb + 1]),
                       reads=[PS(s), "kbias"], writes=[("E", s, bi)])
                for s in act_:
                    qb, kb, gq, first, last = info[s]
                    E_ = Et[s][bi]
                    if first:
                        op("dve", lambda e: e.tensor_tensor(out=E_[:], in0=E_[:], in1=mask4[:], op=ALU.mult),
                           reads=[("E", s, bi), "mask4"], writes=[("E", s, bi)])
                for s in act_:
                    E_, L_ = Et[s][bi], Lt[s][bi]
                    op("act", lambda e: e.activation(out=L_[:], in_=E_[:], func=AF.Ln, scale=1.0, bias=one_t[:]),
                       reads=[("E", s, bi), "one"], writes=[("L", s, bi)])
                for s in act_:
                    qb, kb, gq, first, last = info[s]
                    L_ = Lt[s][bi]
                    ci = 2 + s
                    op("pe", lambda e: e.matmul(ps[ci][:], lhsT=Tm, rhs=L_[:], start=first, stop=True,
                                                skip_group_check=True),
                       reads=[("L", s, bi), "cm_b"], writes=[PS(ci)])
                for s in range(NCH):
                    if nxt[s] is not None:
                        emit_z(s, *nxt[s])
                for s in act_:
                    X_ = Xt[s][bi]
                    ci = 2 + s
                    op("act", lambda e: e.activation(out=X_[:], in_=ps[ci][:], func=AF.Exp, scale=-1.0),
                       reads=[PS(ci)], writes=[("X", s, bi)])
                for s in act_:
                    qb, kb, gq, first, last = info[s]
                    L_ = Lt[s][bi]
                    ci = 2 + s
                    if not last:
                        op("pe", lambda e: e.matmul(ps[ci][:], lhsT=Um, rhs=L_[:], start=False, stop=True,
                                                    skip_group_check=True),
                           reads=[("L", s, bi), "cm_b"], writes=[PS(ci)])
                for s in act_:
                    E_, X_, W_ = Et[s][bi], Xt[s][bi], Wt[s][bi]
                    op("dve", lambda e: e.tensor_tensor(out=W_[:], in0=E_[:], in1=X_[:], op=ALU.mult),
                       reads=[("E", s, bi), ("X", s, bi)], writes=[("W", s, bi)])
                for s in act_:
                    qb, kb, gq, first, last = info[s]
                    W_ = Wt[s][bi]
                    yi = 4 + s
                    for h4 in range(4):
                        pr = h4 // 2
                        op("pe", lambda e: e.matmul(ps[yi][:, h4 * 128:(h4 + 1) * 128],
                                                    lhsT=vS[:, kb, pr * 128:(pr + 1) * 128],
                                                    rhs=W_[:, h4 * 128:(h4 + 1) * 128],
                                                    start=(first and h4 == 0), stop=True, skip_group_check=True),
                           reads=[("vS", kb), ("W", s, bi)], writes=[PS(yi)])
                for s in act_:
                    qb, kb, gq, first, last = info[s]
                    yi = 4 + s
                    if last:
                        for h4 in range(4):
                            pr, half = h4 // 2, h4 % 2
                            rows = slice(half * 64, half * 64 + 64)
                            evac_copy(ysbT[rows, hg * 2 + pr, qb * 128:(qb + 1) * 128], ps[yi][rows, h4 * 128:(h4 + 1) * 128],
                                      [PS(yi)], [("ysbT", hg * 2 + pr, qb, half)])
                cur = nxt
        T.barrier()
        A.release(mB)
        if "ysbT" in debug:
            d = dbg_out("ysbT", [128, 4 * NOWN], BF16)
            dma(d, ysbT[:].rearrange("p a b -> p (a b)"), reads=[])

    def norm_block(X, xkey, H, hkey, t, gbc, gkey, ssT, rsT, rstdT, tag, valid_ap=None):
        op("act", lambda e: e.activation(out=junk2[:], in_=X[:], func=AF.Square, accum_out=ssT[:, t:t + 1]),
           reads=[xkey], writes=[(tag, "ss", t)])
        op("act", lambda e: e.activation(out=rsT[:, t:t + 1], in_=ssT[:, t:t + 1], func=AF.Sqrt,
                                         scale=1.0 / D, bias=eps_t[:]),
           reads=[(tag, "ss", t), "eps"], writes=[(tag, "rs", t)])
        op("dve", lambda e: e.reciprocal(out=rstdT[:, t:t + 1], in_=rsT[:, t:t + 1]),
           reads=[(tag, "rs", t)], writes=[(tag, "rstd", t)])
        if valid_ap is not None:
            op("dve", lambda e: e.tensor_tensor(out=rstdT[:, t:t + 1], in0=rstdT[:, t:t + 1], in1=valid_ap, op=ALU.mult),
               reads=[(tag, "rstd", t), "valid"], writes=[(tag, "rstd", t)])
        op("dve", lambda e: e.scalar_tensor_tensor(out=H[:], in0=X[:], scalar=rstdT[:, t:t + 1], in1=gbc[:],
                                                   op0=ALU.mult, op1=ALU.mult),
           reads=[xkey, (tag, "rstd", t), gkey], writes=[hkey])

    if upi >= 2:
        ymlT = A.alloc("ymlT", [128, 4, NOWN], BF16)
        mC0 = A.mark()
        gT = A.alloc("gT", [128, NT, 16], F32)
        cw = A.alloc("cw", [128, 8, 5], F32)
        gout = A.alloc("gout", [128, 512], F32)
        dma(cw[:].rearrange("p a b -> p (a b)"), cw_d, writes=["cw"])
        dma(gout[:], gout_d, writes=["gout"])
        mC = A.mark()
        LNS = float(np.log(1.0 / np.sqrt(128.0)))
        G = [A.alloc(f"G{i}", [4, TOK], F32) for i in range(5)]
        wgs = WStage("wsg", 8, 8, nbuf=1)
        gb = A.alloc("gb", [4, 2], F32)
        ngb = A.alloc("ngb", [4, 2], F32)
        lns_t = A.alloc("lns", [4, 1], F32)
        ones4 = A.alloc("ones4", [4, 128], F32)
        amax = A.alloc("amax", [4, NT], F32)
        m_after = A.alloc("m_after", [4, NT], F32)
        m_st = A.alloc("m_st", [4, NT], F32)
        mmax = A.alloc("mmax", [4, NT], F32)
        dtmp = A.alloc("dtmp", [4, NT], F32)
        dma(gb[:], gb_d, writes=["gb"])
        op("pool", lambda e: e.memset(lns_t[:], LNS), writes=["lns"])
        op("pool", lambda e: e.memset(ones4[:], 1.0), writes=["ones4"])
        op("pool", lambda e: e.tensor_scalar(out=ngb[:], in0=gb[:], scalar1=-1.0, scalar2=None, op0=ALU.mult),
           reads=["gb"], writes=["ngb"])
        wgb, wgk = wgs.load(wg_d)

        def gate_proj(col0, Gd, gkey):
            for (ta, o, w) in tok_tiles(0, TOK):
                pi = next_ps()
                for kc in range(8):
                    op("pe", lambda e: e.matmul(ps[pi][0:4, :w], lhsT=wgb[:, kc, col0:col0 + 4], rhs=hT[:, kc, ta:ta + w],
                                                start=(kc == 0), stop=(kc == 7)),
                       reads=[wgk] + hT_keys(ta, w), writes=[PS(pi)])
                evac_copy(Gd[:, o:o + w], ps[pi][0:4, :w], [PS(pi)], [gkey])

        def v3(t):
            return t[:].rearrange("p (c t) -> p c t", t=128)

        def bc3(t):
            return t[:].unsqueeze(2).to_broadcast([4, NT, 128])

        gate_proj(4, G[0], "G0")
        dma(G[2][:], ifb_d[:, TOK:2 * TOK], writes=["G2"])
        op("dve", lambda e: e.tensor_tensor(out=G[0][:], in0=G[0][:], in1=G[2][:], op=ALU.add),
           reads=["G0", "G2"], writes=["G0"])
        op("act", lambda e: e.activation(out=G[0][:], in_=G[0][:], func=AF.Exp, scale=-1.0, bias=ngb[:, 1:2]),
           reads=["G0", "ngb"], writes=["G0"])
        op("act", lambda e: e.activation(out=G[0][:], in_=G[0][:], func=AF.Ln, scale=1.0, bias=one_t[0:4, :]),
           reads=["G0", "one"], writes=["G0"])
        for c in range(NT):
            op("dve", lambda e: e.tensor_tensor_scan(out=G[1][:, c * 128:(c + 1) * 128], data0=ones4[:],
                                                     data1=G[0][:, c * 128:(c + 1) * 128], initial=0.0,
                                                     op0=ALU.mult, op1=ALU.add),
               reads=["G0", "ones4"], writes=["G1"])
        gate_proj(0, G[0], "G0")
        dma(G[2][:], ifb_d[:, 0:TOK], reads=[], writes=["G2"])
        op("dve", lambda e: e.tensor_tensor(out=G[0][:], in0=G[0][:], in1=G[2][:], op=ALU.add),
           reads=["G0", "G2"], writes=["G0"])
        op("dve", lambda e: e.scalar_tensor_tensor(out=G[0][:], in0=G[0][:], scalar=gb[:, 0:1], in1=G[1][:],
                                                   op0=ALU.add, op1=ALU.add),
           reads=["G0", "gb", "G1"], writes=["G0"])
        op("dve", lambda e: e.tensor_reduce(out=amax[:], in_=v3(G[0]), axis=AX.X, op=ALU.max),
           reads=["G0"], writes=["amax"])
        nblast = v3(G[1])[:, :, 127]
        op("dve", lambda e: e.tensor_tensor_scan(out=m_after[:], data0=amax[:], data1=nblast, initial=0.0,
                                                 op0=ALU.max, op1=ALU.subtract),
           reads=["amax", "G1"], writes=["m_after"])
        op("pool", lambda e: e.memset(m_st[:], 0.0), writes=["m_st"])
        op("dve", lambda e: e.tensor_copy(out=m_st[:, 1:NT], in_=m_after[:, 0:NT - 1]),
           reads=["m_after", "m_st"], writes=["m_st"])
        op("dve", lambda e: e.tensor_tensor(out=mmax[:], in0=m_st[:], in1=amax[:], op=ALU.max),
           reads=["m_st", "amax"], writes=["mmax"])
        op("dve", lambda e: e.tensor_tensor(out=v3(G[2]), in0=v3(G[0]), in1=bc3(m_st), op=ALU.subtract),
           reads=["G0", "m_st"], writes=["G2"])
        op("act", lambda e: e.activation(out=G[2][:], in_=G[2][:], func=AF.Exp, scale=1.0, bias=lns_t[:]),
           reads=["G2", "lns"], writes=["G2"])
        op("dve", lambda e: e.tensor_tensor(out=v3(G[3]), in0=v3(G[1]), in1=bc3(m_st), op=ALU.subtract),
           reads=["G1", "m_st"], writes=["G3"])
        op("act", lambda e: e.activation(out=G[3][:], in_=G[3][:], func=AF.Exp),
           reads=["G3"], writes=["G3"])
        op("dve", lambda e: e.tensor_tensor(out=v3(G[4]), in0=v3(G[0]), in1=bc3(mmax), op=ALU.subtract),
           reads=["G0", "mmax"], writes=["G4"])
        op("act", lambda e: e.activation(out=G[4][:], in_=G[4][:], func=AF.Exp, scale=1.0, bias=lns_t[:]),
           reads=["G4", "lns"], writes=["G4"])
        op("dve", lambda e: e.tensor_tensor(out=dtmp[:], in0=m_st[:], in1=mmax[:], op=ALU.subtract),
           reads=["m_st", "mmax"], writes=["dtmp"])
        op("act", lambda e: e.activation(out=dtmp[:], in_=dtmp[:], func=AF.Exp), reads=["dtmp"], writes=["dtmp"])
        op("dve", lambda e: e.tensor_copy(out=v3(G[1]), in_=bc3(dtmp)), reads=["dtmp", "G3"], writes=["G1"])
        for c in range(NT):
            pi = next_ps()
            for j, gi in enumerate((2, 3, 4, 1)):
                op("pe", lambda e: e.transpose(out=ps[pi][:, j * 4:(j + 1) * 4], in_=G[gi][0:4, c * 128:(c + 1) * 128],
                                               identity=ident_f[0:4, 0:4]),
                   reads=[f"G{gi}", "ident_f"], writes=[PS(pi)])
            evac_copy(gT[:, c, :], ps[pi][:, 0:16], [PS(pi)], [("gT", c)])
        if "gT" in debug:
            d = dbg_out("gT", [128, NT * 16], F32)
            dma(d, gT[:].rearrange("p a b -> p (a b)"), reads=[("gT", c) for c in range(NT)])
        T.barrier()
        A.release(mC)
        preb = A.alloc("preb", [128, 3 + TOK], F32)
        accb = A.alloc("accb", [128, TOK], F32)
        preq = A.alloc("preq", [128, 3 + NOWN], F32)
        accq = A.alloc("accq", [128, NOWN], F32)
        kTh = A.alloc("kTh", [128, TOK], BF16)
        qTh = A.alloc("qTh", [128, NOWN], BF16)
        vaug = A.alloc("vaug", [128, NT, 130], BF16)
        osig = A.alloc("osig", [128, NQ, 128], BF16)
        Cf = A.alloc("Cf", [128, 130], F32)
        Cb2 = [A.alloc(f"Cb{i}", [128, 130], BF16) for i in range(2)]
        kwa = A.alloc("kwa", [128, NT, 128], BF16)
        Sh = [A.alloc(f"Sh{i}", [128, 128], BF16) for i in range(4)]
        hm = [A.alloc(f"hm{i}", [128, 128], F32) for i in range(4)]
        y1 = [A.alloc(f"y1{i}", [128, 128], F32) for i in range(4)]
        y2 = [A.alloc(f"y2{i}", [128, 128], BF16) for i in range(4)]
        junkc = A.alloc("junkc", [128, 128], BF16)
        sm = A.alloc("sm", [128, 2 * NT * 4], F32)
        wsC = WStage("wsC", 8, 128, nbuf=4)
        op("pool", lambda e: e.memset(preb[:, 0:3], 0.0), writes=["preb"])
        op("pool", lambda e: e.memset(preq[:, 0:3], 0.0), writes=["preq"])
        op("pool", lambda e: e.memset(vaug[:, :, 128:130], 1.0), writes=["vaug1"])

        def conv4(n, j, pre_, pkey, acc_, akey):
            N = n
            op("dve", lambda e: e.tensor_scalar(out=acc_[:, :N], in0=pre_[:, 3:3 + N], scalar1=cw[:, j, 3:4],
                                                scalar2=cw[:, j, 4:5], op0=ALU.mult, op1=ALU.add),
               reads=[pkey, "cw"], writes=[akey])
            for k in range(3):
                op("dve", lambda e: e.scalar_tensor_tensor(out=acc_[:, :N], in0=pre_[:, k:k + N], scalar=cw[:, j, k:k + 1],
                                                           in1=acc_[:, :N], op0=ALU.mult, op1=ALU.add),
                   reads=[pkey, "cw", akey], writes=[akey])

        for h in range(4):
            wb, wk_ = wsC.load(w_in[:, C_MLK + h * 128: C_MLK + (h + 1) * 128])
            proj_fm(wb, wk_, 8, 0, hT, hT_keys, 0, TOK,
                    lambda o, w, p, pk: evac_copy(preb[:, 3 + o:3 + o + w], p, [pk], ["preb"]))
            conv4(TOK, 4 + h, preb, "preb", accb, "accb")
            op("act", lambda e: e.activation(out=kTh[:], in_=accb[:], func=AF.Silu), reads=["accb"], writes=["kTh"])
            wb, wk_ = wsC.load(w_in[:, C_MLQ + h * 128: C_MLQ + (h + 1) * 128])
            def q_evac2(o, w, p, pk):
                op("act", lambda e: e.copy(out=preq[:, 3 + o:3 + o + w], in_=p), reads=[pk], writes=["preq"])
            proj_fm(wb, wk_, 8, 0, hT, hT_keys, OWN0, NOWN, q_evac2)
            conv4(NOWN, h, preq, "preq", accq, "accq")
            op("act", lambda e: e.activation(out=qTh[:], in_=accq[:], func=AF.Silu), reads=["accq"], writes=["qTh"])
            wb, wk_ = wsC.load(w_in[:, C_MLV + h * 128: C_MLV + (h + 1) * 128])
            for t in range(NT):
                pi = next_ps()
                for kc in range(8):
                    op("pe", lambda e: e.matmul(ps[pi][:, :128], lhsT=hT[:, kc, t * 128:(t + 1) * 128], rhs=wb[:, kc, :128],
                                                start=(kc == 0), stop=(kc == 7)),
                       reads=[wk_, ("hT", t)], writes=[PS(pi)])
                evac_copy(vaug[:, t, 0:128], ps[pi][:, :128], [PS(pi)], [("vaug", t)])
            wb, wk_ = wsC.load(w_in[:, C_MLO + h * 128: C_MLO + (h + 1) * 128])
            for qb in range(NQ):
                t = NP + qb
                pi = next_ps()
                for kc in range(8):
                    op("pe", lambda e: e.matmul(ps[pi][:, :128], lhsT=hT[:, kc, t * 128:(t + 1) * 128], rhs=wb[:, kc, :128],
                                                start=(kc == 0), stop=(kc == 7)),
                       reads=[wk_, ("hT", t)], writes=[PS(pi)])
                op("act", lambda e: e.activation(out=osig[:, qb, :], in_=ps[pi][:, :128], func=AF.Sigmoid),
                   reads=[PS(pi)], writes=[("osig", qb)])
            op("pool", lambda e: e.memset(Cf[:], 0.0), writes=["Cf"])
            op("pool", lambda e: e.memset(Cb2[1][:], 0.0), writes=[("Cb", 1)])
            for b0 in range(0, NT - 1, 8):
                bk = (b0 // 8) % 2
                cl = list(range(b0, min(b0 + 8, NT - 1)))
                for c in cl:
                    cs = slice(c * 128, (c + 1) * 128)
                    op("pe", lambda e: e.transpose(out=pb[bk][:, c % 8, :], in_=kTh[:, cs], identity=ident_b[:]),
                       reads=["kTh", "ident_b"], writes=[PB(bk)])
                for c in cl:
                    op("act", lambda e: e.activation(out=kwa[:, c, :], in_=pb[bk][:, c % 8, :], func=AF.Copy,
                                                     scale=gT[:, c, 8 + h:9 + h]),
                       reads=[PB(bk), "gTall"], writes=[("kwa", c)])
            pNs = {}
            for it in range(NT + 3):
                c = it
                if c < NT:
                    own = c >= NP
                    qb = c - NP
                    cs = slice(c * 128, (c + 1) * 128)
                    qs = slice(qb * 128, (qb + 1) * 128)
                    i4 = c % 4
                    if c < NT - 1:
                        pU = next_ps()
                        op("pe", lambda e: e.matmul(ps[pU][:, :129], lhsT=kwa[:, c, :], rhs=vaug[:, c, 0:129], start=True, stop=True),
                           reads=[("kwa", c), ("vaug", c), "vaug1"], writes=[PS(pU)])
                    if own:
                        pS = next_ps()
                        op("pe", lambda e: e.matmul(ps[pS][:, :128], lhsT=kTh[:, cs], rhs=qTh[:, qs], start=True, stop=True),
                           reads=["kTh", "qTh"], writes=[PS(pS)])
                        op("dve", lambda e: e.scalar_tensor_tensor(out=Sh[i4][:], in0=ps[pS][:, :128], scalar=gT[:, c, h:h + 1],
                                                                   in1=m_incl, op0=ALU.mult, op1=ALU.mult),
                           reads=[PS(pS), "gTall", "cm_f"], writes=[("Sh", i4)])
                        pN = next_ps()
                        pNs[c] = pN
                        op("pe", lambda e: e.matmul(ps[pN][:, :129], lhsT=Sh[i4][:], rhs=vaug[:, c, 0:129], start=True, stop=False),
                           reads=[("Sh", i4), ("vaug", c), "vaug1"], writes=[PS(pN)])
                        op("pe", lambda e: e.matmul(ps[pN][:, :129], lhsT=qTh[:, qs], rhs=Cb2[(c - 1) % 2][:, 0:129],
                                                    start=False, stop=True),
                           reads=["qTh", ("Cb", (c - 1) % 2)], writes=[PS(pN)])
                    if c < NT - 1:
                        op("dve", lambda e: e.scalar_tensor_tensor(out=Cb2[c % 2][:, :129], in0=Cf[:, :129],
                                                                   scalar=gT[:, c, 12 + h:13 + h], in1=ps[pU][:, :129],
                                                                   op0=ALU.mult, op1=ALU.add),
                           reads=["Cf", "gTall", PS(pU)], writes=[("Cb", c % 2)])
                        op("dve", lambda e: e.scalar_tensor_tensor(out=Cf[:, :129], in0=Cf[:, :129],
                                                                   scalar=gT[:, c, 12 + h:13 + h], in1=ps[pU][:, :129],
                                                                   op0=ALU.mult, op1=ALU.add),
                           reads=["Cf", "gTall", PS(pU)], writes=["Cf"])
                ca = it - 1
                if NP <= ca < NT:
                    c = ca
                    i4 = c % 4
                    pN = pNs[c]
                    k0 = (c * 4 + h) * 2
                    dn = sm[:, k0:k0 + 1]
                    s1 = sm[:, k0 + 1:k0 + 2]
                    op("dve", lambda e: e.tensor_scalar(out=s1, in0=ps[pN][:, 128:129], scalar1=-1.0,
                                                        scalar2=gT[:, c, 4 + h:5 + h], op0=ALU.mult, op1=ALU.max),
                       reads=[PS(pN), "gTall"], writes=[("sm1", k0)])
                    op("dve", lambda e: e.tensor_scalar(out=dn, in0=ps[pN][:, 128:129], scalar1=s1,
                                                        scalar2=None, op0=ALU.max),
                       reads=[PS(pN), ("sm1", k0)], writes=[("sm", k0)])
                    op("dve", lambda e: e.reciprocal(out=dn, in_=dn), reads=[("sm", k0)], writes=[("sm", k0)])
                    op("dve", lambda e: e.tensor_scalar(out=hm[i4][:], in0=ps[pN][:, :128], scalar1=dn, scalar2=None,
                                                        op0=ALU.mult),
                       reads=[PS(pN), ("sm", k0)], writes=[("hm", i4)])
                    op("act", lambda e: e.activation(out=junkc[:], in_=hm[i4][:], func=AF.Square, accum_out=s1),
                       reads=[("hm", i4)], writes=[("sm1", k0)])
                    op("act", lambda e: e.activation(out=s1, in_=s1, func=AF.Sqrt, scale=1.0 / 128, bias=eps_t[:]),
                       reads=[("sm1", k0), "eps"], writes=[("sm1", k0)])
                cb_ = it - 2
                if NP <= cb_ < NT:
                    c = cb_
                    i4 = c % 4
                    qb = c - NP
                    k0 = (c * 4 + h) * 2
                    s1 = sm[:, k0 + 1:k0 + 2]
                    op("dve", lambda e: e.reciprocal(out=s1, in_=s1), reads=[("sm1", k0)], writes=[("sm1", k0)])
                    op("dve", lambda e: e.scalar_tensor_tensor(out=y1[i4][:], in0=hm[i4][:], scalar=s1,
                                                               in1=gout[:, h * 128:(h + 1) * 128], op0=ALU.mult, op1=ALU.mult),
                       reads=[("hm", i4), ("sm1", k0), "gout"], writes=[("y1", i4)])
                    op("dve", lambda e: e.tensor_tensor(out=y2[i4][:], in0=y1[i4][:], in1=osig[:, qb, :], op=ALU.mult),
                       reads=[("y1", i4), ("osig", qb)], writes=[("y2", i4)])
                    op("pe", lambda e: e.transpose(out=pb[c % 2][:, 0, :], in_=y2[i4][:], identity=ident_b[:]),
                       reads=[("y2", i4), "ident_b"], writes=[PB(c % 2)])
                cc = it - 3
                if NP <= cc < NT:
                    c = cc
                    qb = c - NP
                    qs = slice(qb * 128, (qb + 1) * 128)
                    op("act", lambda e: e.copy(out=ymlT[:, h, qs], in_=pb[c % 2][:, 0, :]),
                       reads=[PB(c % 2)], writes=[("ymlT", h, qb)])
        T.barrier()
        A.release(mC0)
        if "ymlT" in debug:
            d = dbg_out("ymlT", [128, 4 * NOWN], BF16)
            dma(d, ymlT[:].rearrange("p a b -> p (a b)"), reads=[])

    if upi >= 3:
        yxaT = A.alloc("yxaT", [128, 4, NOWN], BF16)
        mD = A.mark()
        junk2 = A.alloc("junk2", [128, D], BF16)
        gmem = A.alloc("gmem", [128, D], F32)
        xag = A.alloc("xag", [128, 2], F32)
        dma(gmem[:], g_mem_bc, writes=["gmem"])
        dma(xag[:], xa_g_d, writes=["xag"])
        mnT = A.alloc("mnT", [128, 8, NMEM], BF16)
        xm = [A.alloc(f"xm{i}", [128, D], F32) for i in range(2)]
        hm_ = [A.alloc(f"hmm{i}", [128, D], BF16) for i in range(2)]
        ssD = A.alloc("ssD", [128, 2], F32)
        rsD = A.alloc("rsD", [128, 2], F32)
        rstdD = A.alloc("rstdD", [128, 2], F32)
        for mt in range(2):
            dma(xm[mt][:], mem[mt * 128:(mt + 1) * 128, :], writes=[("xm", mt)])
            norm_block(xm[mt], ("xm", mt), hm_[mt], ("hmm", mt), mt, gmem, "gmem", ssD, rsD, rstdD, "D")
            for kc in range(8):
                op("pe", lambda e: e.transpose(out=pb[mt][:, kc, :], in_=hm_[mt][:, kc * 128:(kc + 1) * 128], identity=ident_b[:]),
                   reads=[("hmm", mt), "ident_b"], writes=[PB(mt)])
            evac_copy(mnT[:, :, mt * 128:(mt + 1) * 128], pb[mt][:], [PB(mt)], ["mnT"])
        wsD = WStage("wsD", 8, 512, nbuf=2)
        khT = A.alloc("khT", [128, 4, NMEM], BF16)
        vX = A.alloc("vX", [128, 2, 512], BF16)
        raw = A.alloc("raw", [128, 512], F32)
        sq = A.alloc("sq", [128, 512], BF16)
        rsb = A.alloc("rsb", [128, 512], F32)
        qn = A.alloc("qn", [128, 512], BF16)
        Pm = A.alloc("Pm", [128, 2, 512], BF16)
        rden = A.alloc("rden", [128, 512], F32)

        def qknorm(pi, w, gcol, out_ap, out_key):
            op("act", lambda e: e.copy(out=raw[:, :w], in_=ps[pi][:, :w]), reads=[PS(pi)], writes=["raw"])
            op("act", lambda e: e.activation(out=sq[:, :w], in_=ps[pi][:, :w], func=AF.Square), reads=[PS(pi)], writes=["sq"])
            pj = next_ps()
            op("pe", lambda e: e.matmul(ps[pj][:, :w], lhsT=ones_b[:], rhs=sq[:, :w], start=True, stop=True),
               reads=["ones_b", "sq"], writes=[PS(pj)])
            op("act", lambda e: e.activation(out=rsb[:, :w], in_=ps[pj][:, :w], func=AF.Sqrt, scale=1.0 / 128, bias=eps_t[:]),
               reads=[PS(pj), "eps"], writes=["rsb"])
            op("dve", lambda e: e.reciprocal(out=rsb[:, :w], in_=rsb[:, :w]), reads=["rsb"], writes=["rsb"])
            op("dve", lambda e: e.scalar_tensor_tensor(out=out_ap, in0=raw[:, :w], scalar=xag[:, gcol:gcol + 1], in1=rsb[:, :w],
                                                       op0=ALU.mult, op1=ALU.mult),
               reads=["raw", "xag", "rsb"], writes=[out_key])

        wb, wk_ = wsD.load(w_kv[:, 0:512])
        for h in range(4):
            pi = next_ps()
            for kc in range(8):
                op("pe", lambda e: e.matmul(ps[pi][:, :NMEM], lhsT=wb[:, kc, h * 128:(h + 1) * 128], rhs=mnT[:, kc, :],
                                            start=(kc == 0), stop=(kc == 7)),
                   reads=[wk_, "mnT"], writes=[PS(pi)])
            qknorm(pi, NMEM, 1, khT[:, h, :], ("khT", h))
        wb, wk_ = wsD.load(w_kv[:, 512:1024])
        for mt in range(2):
            pi = next_ps()
            for kc in range(8):
                op("pe", lambda e: e.matmul(ps[pi][:, :512], lhsT=mnT[:, kc, mt * 128:(mt + 1) * 128], rhs=wb[:, kc, :512],
                                            start=(kc == 0), stop=(kc == 7)),
                   reads=[wk_, "mnT"], writes=[PS(pi)])
            evac_copy(vX[:, mt, :], ps[pi][:, :512], [PS(pi)], [("vX", mt)])
        wsDq = WStage("wsDq", 8, 128, nbuf=3)
        SC = float(1.0 / np.sqrt(128.0))
        for h in range(4):
            wb, wk_ = wsDq.load(w_in[:, C_XAQ + h * 128: C_XAQ + (h + 1) * 128])
            for (ta, o, w) in tok_tiles(OWN0, NOWN):
                pi = next_ps()
                for kc in range(8):
                    op("pe", lambda e: e.matmul(ps[pi][:, :w], lhsT=wb[:, kc, :128], rhs=hT[:, kc, ta:ta + w],
                                                start=(kc == 0), stop=(kc == 7)),
                       reads=[wk_] + hT_keys(ta, w), writes=[PS(pi)])
                qknorm(pi, w, 0, qn[:, :w], "qn")
                po = next_ps()
                pd = next_ps()
                for mt in range(2):
                    pz = next_ps()
                    op("pe", lambda e: e.matmul(ps[pz][:, :w], lhsT=khT[:, h, mt * 128:(mt + 1) * 128], rhs=qn[:, :w],
                                                start=True, stop=True),
                       reads=[("khT", h), "qn"], writes=[PS(pz)])
                    op("act", lambda e: e.activation(out=Pm[:, mt, :w], in_=ps[pz][:, :w], func=AF.Exp, scale=SC),
                       reads=[PS(pz)], writes=[("Pm", mt)])
                for mt in range(2):
                    op("pe", lambda e: e.matmul(ps[po][:, :w], lhsT=vX[:, mt, h * 128:(h + 1) * 128], rhs=Pm[:, mt, :w],
                                                start=(mt == 0), stop=(mt == 1)),
                       reads=[("vX", mt), ("Pm", mt)], writes=[PS(po)])
                for mt in range(2):
                    op("pe", lambda e: e.matmul(ps[pd][:, :w], lhsT=ones_b[:], rhs=Pm[:, mt, :w],
                                                start=(mt == 0), stop=(mt == 1)),
                       reads=["ones_b", ("Pm", mt)], writes=[PS(pd)])
                op("dve", lambda e: e.reciprocal(out=rden[:, :w], in_=ps[pd][:, :w]), reads=[PS(pd)], writes=["rden"])
                op("dve", lambda e: e.tensor_tensor(out=yxaT[:, h, o:o + w], in0=ps[po][:, :w], in1=rden[:, :w], op=ALU.mult),
                   reads=[PS(po), "rden"], writes=[("yxaT", h, o)])
        T.barrier()
        A.release(mD)
        if "yxaT" in debug:
            d = dbg_out("yxaT", [128, 4 * NOWN], BF16)
            dma(d, yxaT[:].rearrange("p a b -> p (a b)"), reads=[])

    if upi >= 4:
        mergedT = A.alloc_top("mergedT", [128, 8, NOWN], BF16)
        mE = A.mark()
        wsG = WStage("wsEg", 8, 128, nbuf=4)
        wsO = WStage("wsEo", 4, 128, nbuf=4)
        gsb = [A.alloc(f"gsb{i}", [128, 512], F32) for i in range(2)]
        tmpE = [A.alloc(f"tmpE{i}", [128, 512], F32) for i in range(2)]
        accE = A.alloc("accE", [128, NOWN], F32)
        ybr = [ysbT, ymlT, yxaT]
        u = 0
        for ct in range(8):
            for br in range(3):
                c0 = C_GATE + br * 1024 + ct * 128
                wg, wgk_ = wsG.load(w_in[:, c0:c0 + 128])
                wo_, wok_ = wsO.load(wbr_d[br][:, ct * 128:(ct + 1) * 128])
                for (ta, o, w) in tok_tiles(0, NOWN):
                    i2 = u % 2
                    u += 1
                    pg = next_ps()
                    for kc in range(8):
                        op("pe", lambda e: e.matmul(ps[pg][:, :w], lhsT=wg[:, kc, :128], rhs=hT[:, kc, OWN0 + o:OWN0 + o + w],
                                                    start=(kc == 0), stop=(kc == 7)),
                           reads=[wgk_], writes=[PS(pg)])
                    pp = next_ps()
                    for kc in range(4):
                        op("pe", lambda e: e.matmul(ps[pp][:, :w], lhsT=wo_[:, kc, :128], rhs=ybr[br][:, kc, o:o + w],
                                                    start=(kc == 0), stop=(kc == 3)),
                           reads=[wok_], writes=[PS(pp)])
                    op("act", lambda e: e.activation(out=gsb[i2][:, :w], in_=ps[pg][:, :w], func=AF.Sigmoid),
                       reads=[PS(pg)], writes=[("gsb", i2)])
                    if br == 0:
                        op("dve", lambda e: e.tensor_tensor(out=accE[:, o:o + w], in0=ps[pp][:, :w], in1=gsb[i2][:, :w], op=ALU.mult),
                           reads=[PS(pp), ("gsb", i2)], writes=[("accE", o)])
                    else:
                        op("dve", lambda e: e.tensor_tensor(out=tmpE[i2][:, :w], in0=ps[pp][:, :w], in1=gsb[i2][:, :w], op=ALU.mult),
                           reads=[PS(pp), ("gsb", i2)], writes=[("tmpE", i2)])
                        if br == 1:
                            op("dve", lambda e: e.tensor_tensor(out=accE[:, o:o + w], in0=accE[:, o:o + w], in1=tmpE[i2][:, :w], op=ALU.add),
                               reads=[("accE", o), ("tmpE", i2)], writes=[("accE", o)])
                        else:
                            op("dve", lambda e: e.tensor_tensor(out=mergedT[:, ct, o:o + w], in0=accE[:, o:o + w], in1=tmpE[i2][:, :w], op=ALU.add),
                               reads=[("accE", o), ("tmpE", i2)], writes=[("mergedT", ct, o)])
        T.barrier()
        if "mergedT" in debug:
            d = dbg_out("mergedT", [128, 8 * NOWN], BF16)
            dma(d, mergedT[:].rearrange("p a b -> p (a b)"), reads=[])
        A.release(mConst)
        h2T = A.alloc("h2T", [128, 8, NOWN], BF16)
        mE2 = A.mark()
        junk2 = A.alloc("junk2", [128, D], BF16)
        wo_b = A.alloc("wo_b", [128, 8, D], BF16)
        gffn = A.alloc("gffn", [128, D], F32)
        valid = A.alloc("valid", [128, NQ], F32)
        dma(gffn[:], g_ffn_bc, writes=["gffn"])
        dma(valid[:], valid_d, writes=["valid"])
        for half in range(2):
            dma(wo_b[:, :, half * 512:(half + 1) * 512], w_o_d[:, half * 512:(half + 1) * 512].rearrange("(kc p) n -> p kc n", p=128),
                writes=["wo_b"], q="pool")
        xe = [A.alloc(f"xe{i}", [128, D], F32) for i in range(3)]
        x1t = [A.alloc(f"x1t{i}", [128, D], F32) for i in range(3)]
        hbe = [A.alloc(f"hbe{i}", [128, D], BF16) for i in range(3)]
        ssE = A.alloc("ssE", [128, NQ], F32)
        rsE = A.alloc("rsE", [128, NQ], F32)
        rstdE = A.alloc("rstdE", [128, NQ], F32)
        def e2_front(qb):
            i2 = qb % 3
            dma(xe[i2][:], xs[OWN0 + qb * 128:OWN0 + (qb + 1) * 128, :], writes=[("xe", i2)])
            for half in range(2):
                pi = next_ps()
                for kc in range(8):
                    op("pe", lambda e: e.matmul(ps[pi][:, :512], lhsT=mergedT[:, kc, qb * 128:(qb + 1) * 128],
                                                rhs=wo_b[:, kc, half * 512:(half + 1) * 512], start=(kc == 0), stop=(kc == 7)),
                       reads=["wo_b"], writes=[PS(pi)])
                op("dve", lambda e: e.tensor_tensor(out=x1t[i2][:, half * 512:(half + 1) * 512], in0=ps[pi][:, :512],
                                                    in1=xe[i2][:, half * 512:(half + 1) * 512], op=ALU.add),
                   reads=[PS(pi), ("xe", i2)], writes=[("x1t", i2)])
            dma(x1s[qb * 128:(qb + 1) * 128, :], x1t[i2][:], reads=[("x1t", i2)], writes=[("x1s", qb)])
            op("act", lambda e: e.activation(out=junk2[:], in_=x1t[i2][:], func=AF.Square, accum_out=ssE[:, qb:qb + 1]),
               reads=[("x1t", i2)], writes=[("E", "ss", qb)])
            op("act", lambda e: e.activation(out=rsE[:, qb:qb + 1], in_=ssE[:, qb:qb + 1], func=AF.Sqrt,
                                             scale=1.0 / D, bias=eps_t[:]),
               reads=[("E", "ss", qb), "eps"], writes=[("E", "rs", qb)])

        def e2_back(qb):
            i2 = qb % 3
            ib = qb % 2
            op("dve", lambda e: e.reciprocal(out=rstdE[:, qb:qb + 1], in_=rsE[:, qb:qb + 1]),
               reads=[("E", "rs", qb)], writes=[("E", "rstd", qb)])
            op("dve", lambda e: e.tensor_tensor(out=rstdE[:, qb:qb + 1], in0=rstdE[:, qb:qb + 1], in1=valid[:, qb:qb + 1], op=ALU.mult),
               reads=[("E", "rstd", qb), "valid"], writes=[("E", "rstd", qb)])
            op("dve", lambda e: e.scalar_tensor_tensor(out=hbe[i2][:], in0=x1t[i2][:], scalar=rstdE[:, qb:qb + 1], in1=gffn[:],
                                                       op0=ALU.mult, op1=ALU.mult),
               reads=[("x1t", i2), ("E", "rstd", qb), "gffn"], writes=[("hbe", i2)])
            for kc in range(8):
                op("pe", lambda e: e.transpose(out=pb[ib][:, kc, :], in_=hbe[i2][:, kc * 128:(kc + 1) * 128], identity=ident_b[:]),
                   reads=[("hbe", i2), "ident_b"], writes=[PB(ib)])

        def e2_evac(qb):
            ib = qb % 2
            evac_copy(h2T[:, :, qb * 128:(qb + 1) * 128], pb[ib][:], [PB(ib)], [("h2T", qb)])

        for qb in range(NQ + 2):
            if qb < NQ:
                e2_front(qb)
            if 1 <= qb <= NQ:
                e2_back(qb - 1)
            if qb >= 2:
                e2_evac(qb - 2)
        T.barrier()
        A.release(mE2)
        A.hi = A.hi0
        if "x1" in debug:
            d = dbg_out("x1", [NOWN, D], F32)
            dma(d, x1s, reads=[("x1s", qb) for qb in range(NQ)])

    if upi >= 5:
        NF = 16 * 128
        aT = A.alloc_top("aT", [128, 22, NF], BF16)
        mF = A.mark()
        upv = A.alloc("upv", [128, 2 + NOWN], F32)
        upg = A.alloc("upg", [128, 2 + NOWN], F32)
        acv = [A.alloc(f"acv{i}", [128, NOWN], F32) for i in range(2)]
        acg = [A.alloc(f"acg{i}", [128, NOWN], F32) for i in range(2)]
        cf = A.alloc("cf", [128, 44, 4], F32)
        dma(cf[:].rearrange("p a b -> p (a b)"), cf_d, writes=["cf"])
        wsU = WStage("wsU", 8, 256, nbuf=4)
        op("pool", lambda e: e.memset(upv[:, 0:2], 0.0), writes=["upv"])
        op("pool", lambda e: e.memset(upg[:, 0:2], 0.0), writes=["upg"])

        def h2_keys(ta, w):
            return []

        def act_evac(dst, dkey):
            def f(o, w, p, pk):
                op("act", lambda e: e.copy(out=dst[:, 2 + o:2 + o + w], in_=p), reads=[pk], writes=[dkey])
            return f

        def conv3(src_, skey, dst, dkey, j):
            op("act", lambda e: e.activation(out=dst[:], in_=src_[:, 2:2 + NOWN], func=AF.Identity,
                                             scale=cf[:, j, 2:3], bias=cf[:, j, 3:4]),
               reads=[skey, "cf"], writes=[dkey])
            for k in range(2):
                op("dve", lambda e: e.scalar_tensor_tensor(out=dst[:], in0=src_[:, k:k + NOWN], scalar=cf[:, j, k:k + 1],
                                                           in1=dst[:], op0=ALU.mult, op1=ALU.add),
                   reads=[skey, "cf", dkey], writes=[dkey])

        def tail(j):
            jb = j % 2
            op("act", lambda e: e.activation(out=acg[jb][:], in_=acg[jb][:], func=AF.Silu),
               reads=[("acg", jb)], writes=[("acg", jb)])
            op("dve", lambda e: e.tensor_tensor(out=aT[:, j, :], in0=acg[jb][:, 128:], in1=acv[jb][:, 128:], op=ALU.mult),
               reads=[("acg", jb), ("acv", jb)], writes=[("aT", j)])

        for j in range(22):
            jb = j % 2
            if j % 2 == 0:
                wv, wvk = wsU.load(w_up[:, j * 128:(j + 2) * 128])
                wg, wgk_ = wsU.load(w_up[:, DFF + j * 128:DFF + (j + 2) * 128])
            proj_fm(wv, wvk, 8, (j % 2) * 128, h2T, h2_keys, 0, NOWN, act_evac(upv, "upv"))
            proj_fm(wg, wgk_, 8, (j % 2) * 128, h2T, h2_keys, 0, NOWN, act_evac(upg, "upg"))
            conv3(upv, "upv", acv[jb], ("acv", jb), j)
            conv3(upg, "upg", acg[jb], ("acg", jb), 22 + j)
            if j > 0:
                tail(j - 1)
        tail(21)
        T.barrier()
        A.release(mConst)
        wd_b = A.alloc("wd_b", [128, 22, D], BF16)
        for half in range(2):
            for jj in range(2):
                dma(wd_b[:, jj * 11:(jj + 1) * 11, half * 512:(half + 1) * 512],
                    w_down[jj * 11 * 128:(jj + 1) * 11 * 128, half * 512:(half + 1) * 512].rearrange("(j p) n -> p j n", p=128),
                    writes=["wd_b"], q="pool")
        x1r = [A.alloc(f"x1r{i}", [128, D], F32) for i in range(2)]
        ob = [A.alloc(f"ob{i}", [128, D], F32) for i in range(2)]
        for qb in range(1, NQ):
            i2 = qb % 2
            dma(x1r[i2][:], x1s[qb * 128:(qb + 1) * 128, :], reads=[("x1s", qb)], writes=[("x1r", i2)])
            for half in range(2):
                pi = next_ps()
                for j in range(22):
                    op("pe", lambda e: e.matmul(ps[pi][:, :512], lhsT=aT[:, j, (qb - 1) * 128:qb * 128],
                                                rhs=wd_b[:, j, half * 512:(half + 1) * 512], start=(j == 0), stop=(j == 21)),
                       reads=["wd_b"], writes=[PS(pi)])
                op("dve", lambda e: e.tensor_tensor(out=ob[i2][:, half * 512:(half + 1) * 512], in0=ps[pi][:, :512],
                                                    in1=x1r[i2][:, half * 512:(half + 1) * 512], op=ALU.add),
                   reads=[PS(pi), ("x1r", i2)], writes=[("ob", i2)])
            dma(out_d[(qb - 1) * 128:qb * 128, :], ob[i2][:], reads=[("ob", i2)], writes=[("out", qb)])

    T.finish()
    nc._nw = T.nwait
    return nc, dbg


def _prep_inputs(inputs):
    f = lambda k: np.asarray(inputs[k], np.float32)
    x = f("x")
    maps = []
    ident = np.eye(128, dtype=np.float32)
    ii = np.arange(128)
    strict = (ii[:, None] < ii[None, :]).astype(np.float32)
    incl = (ii[:, None] <= ii[None, :]).astype(np.float32)
    tm = (ii[:, None] >= ii[None, :]).astype(np.float32)
    um = (ii[:, None] < ii[None, :]).astype(np.float32)
    rowmask = np.stack([(ii < 64), (ii >= 64)], axis=1).astype(np.float32)
    cmasks = np.ascontiguousarray(np.concatenate([strict, incl, tm, um], axis=1))
    bc = lambda v: np.ascontiguousarray(np.broadcast_to(np.asarray(v, np.float32).reshape(1, -1), (128, v.size)))
    cwv = f("ml_conv_w")[0]
    cbv = f("ml_conv_b")[0]
    cw = np.zeros((128, 8, 5), np.float32)
    for j in range(8):
        cols = j * 128 + ii
        cw[:, j, 0:4] = cwv[:, cols].T
        cw[:, j, 4] = cbv[cols]
    fw = f("ff_conv_w")[0]
    fb = f("ff_conv_b")[0]
    cf = np.zeros((128, 44, 4), np.float32)
    for j in range(44):
        cols = j * 128 + ii
        cf[:, j, 0:3] = fw[:, cols].T
        cf[:, j, 3] = fb[cols]
    gb = np.stack([f("ml_b_i")[0], f("ml_b_f")[0]], axis=1)
    xa_g = np.stack([f("xa_g_q")[0], f("xa_g_k")[0]], axis=1)
    shared = {
        "w_in": np.ascontiguousarray(f("w_in")[0]),
        "g_mix_bc": bc(f("g_mix")[0]),
        "ident": ident, "cmasks": cmasks, "rowmask": rowmask,
        "g_mem_bc": bc(f("g_mem")[0]),
        "xa_g": np.ascontiguousarray(xa_g),
        "w_mem_kv": np.ascontiguousarray(f("w_mem_kv")[0]),
        "gb": np.ascontiguousarray(gb),
        "cw": np.ascontiguousarray(cw.reshape(128, 40)),
        "gout_bc": bc(f("ml_g_out")[0].reshape(-1)),
        "w_sb_out": np.ascontiguousarray(f("w_sb_out")[0]),
        "w_ml_out": np.ascontiguousarray(f("w_ml_out")[0]),
        "w_xa_out": np.ascontiguousarray(f("w_xa_out")[0]),
        "w_o": np.ascontiguousarray(f("w_o")[0]),
        "g_ffn_bc": bc(f("g_ffn")[0]),
        "w_up": np.ascontiguousarray(f("w_up")[0]),
        "cf": np.ascontiguousarray(cf.reshape(128, 176)),
        "w_down": np.ascontiguousarray(f("w_down")[0]),
    }
    for c in range(8):
        b, p = c // 2, c % 2
        kbias = np.zeros((128, NT), np.float32)
        ifb = np.zeros((4, 2 * TOK), np.float32)
        valid = np.ones((128, NQ), np.float32)
        if p == 1:
            xs = x[b]
        else:
            xs = np.concatenate([np.zeros((2048, D), np.float32), x[b, :2048]], axis=0)
            kbias[:, :16] = NEG
            ifb[:, 0:2048] = NEG
            ifb[:, TOK:TOK + 2048] = -NEG
            valid[:, 0] = 0.0
        m = dict(shared)
        m.update({"xs": np.ascontiguousarray(xs), "mem": np.ascontiguousarray(f("mem")[b]),
                  "kbias": kbias, "ifbias": ifb, "valid": valid})
        maps.append(m)
    return maps


def kernel(**inputs):
    inputs = {k: np.asarray(v) for k, v in inputs.items()}
    nc, _ = build_program()
    maps = _prep_inputs(inputs)
    res = run_bass_kernel_spmd(nc, maps, core_ids=list(range(8)))
    out = np.zeros((4, SEQ, D), np.float32)
    for c in range(8):
        b, p = c // 2, c % 2
        out[b, p * 2048:(p + 1) * 2048] = res.results[c]["out"]
    return out
```
